# Optimizing a Trainium2 kernel written in Bass

```python
import math
import jax, jax.numpy as jnp
from jax import lax
import numpy as np

D_MODEL = 2048
BATCH = 4
SEQ = 4096
DEPTH = 1

HEAD_DIM = 128
Q_BLOCK = 128
A_Q_HEADS = 8
A_KV_HEADS = 2
A_REP = A_Q_HEADS // A_KV_HEADS
ROPE_THETA = 10000.0
ROPE_AXIS_DIM = HEAD_DIM // 2
GRID_W = 64
B_WINDOWS = (128, 512, 2048)
B_DILATIONS = (1, 4, 16)
B_GROUPS = 3
B_HEADS_PER_GROUP = 4
B_HEADS = B_GROUPS * B_HEADS_PER_GROUP
B_SIDE = B_WINDOWS[0] // (2 * B_DILATIONS[0])
B_KEYS = 2 * B_SIDE + 1
A_Q_W = A_Q_HEADS * HEAD_DIM
A_KV_W = A_KV_HEADS * HEAD_DIM
B_W = B_HEADS * HEAD_DIM
IN_WIDTHS = (A_Q_W, A_KV_W, A_KV_W, B_W, B_W, B_W, D_MODEL, D_MODEL)
IN_TOTAL = A_Q_W + 2 * A_KV_W + 3 * B_W + 2 * D_MODEL
A_OUT_W = A_Q_W
B_OUT_W = B_HEADS_PER_GROUP * HEAD_DIM
FFN_HIDDEN = -(-8 * D_MODEL // (3 * 256)) * 256
ALPHA = (2.0 * DEPTH) ** 0.25
BETA = (8.0 * DEPTH) ** -0.25
RMS_EPS = 1e-6
LN_EPS = 1e-5

kernel_name = 'hybrid_gqa_axialrope_dilated_alibi_deepnorm_swiglu'


def _layer_norm(x, g, b):
    x32 = x.astype(jnp.float32)
    mu = jnp.mean(x32, axis=-1, keepdims=True)
    var = jnp.mean(jnp.square(x32 - mu), axis=-1, keepdims=True)
    y = (x32 - mu) * lax.rsqrt(var + LN_EPS) * g.astype(jnp.float32) + b.astype(jnp.float32)
    return y.astype(x.dtype)


def _rms_norm(x, g):
    x32 = x.astype(jnp.float32)
    y = x32 * lax.rsqrt(jnp.mean(jnp.square(x32), axis=-1, keepdims=True) + RMS_EPS)
    return (y * g.astype(jnp.float32)).astype(x.dtype)


def _axial_rope_tables(seq_len):
    rows = seq_len // GRID_W
    row_id = jnp.broadcast_to(jnp.arange(rows)[:, None], (rows, GRID_W)).reshape(-1)
    col_id = jnp.broadcast_to(jnp.arange(GRID_W)[None, :], (rows, GRID_W)).reshape(-1)
    n_freq = ROPE_AXIS_DIM // 2
    freqs = ROPE_THETA ** (-jnp.arange(n_freq, dtype=jnp.float32) / n_freq)
    ang_r = row_id.astype(jnp.float32)[:, None] * freqs[None, :]
    ang_c = col_id.astype(jnp.float32)[:, None] * freqs[None, :]
    return (jnp.cos(ang_r), jnp.sin(ang_r), jnp.cos(ang_c), jnp.sin(ang_c))


def _rotate_half(xh, cos, sin):
    x1, x2 = jnp.split(xh, 2, axis=-1)
    c = cos[None, :, None, :]
    s = sin[None, :, None, :]
    return jnp.concatenate([x1 * c - x2 * s, x2 * c + x1 * s], axis=-1)


def _apply_axial_rope(x, tabs):
    cos_r, sin_r, cos_c, sin_c = tabs
    xr = _rotate_half(x[..., :ROPE_AXIS_DIM], cos_r, sin_r)
    xc = _rotate_half(x[..., ROPE_AXIS_DIM:], cos_c, sin_c)
    return jnp.concatenate([xr, xc], axis=-1).astype(x.dtype)


def _gqa_blocks(q, k, v):
    bsz, seq_len = q.shape[0], q.shape[1]
    nb = seq_len // Q_BLOCK
    qb = q.reshape(bsz, nb, Q_BLOCK, A_KV_HEADS, A_REP, HEAD_DIM).swapaxes(0, 1)

    def one_block(qblk):
        s = jnp.einsum('bqgrd,bkgd->bgrqk', qblk, k, preferred_element_type=jnp.float32)
        p = jax.nn.softmax(s, axis=-1)
        return jnp.einsum('bgrqk,bkgd->bqgrd', p.astype(v.dtype), v)

    o = lax.map(one_block, qb)
    return o.swapaxes(0, 1).reshape(bsz, seq_len, A_OUT_W)


def _dilated_offsets():
    side = jnp.arange(-B_SIDE, B_SIDE + 1, dtype=jnp.int32)
    return jnp.stack([dil * side for dil in B_DILATIONS], axis=0)


def _alibi_slopes():
    i = jnp.arange(1, B_HEADS + 1, dtype=jnp.float32)
    return (2.0 ** (-8.0 * i / B_HEADS)).reshape(B_GROUPS, B_HEADS_PER_GROUP)


def _dilated_window_attention(q, k, v):
    bsz, seq_len = q.shape[0], q.shape[1]
    nb = seq_len // Q_BLOCK
    offs = _dilated_offsets()
    bias = -_alibi_slopes()[:, :, None] * jnp.abs(offs).astype(jnp.float32)[:, None, :]
    g_idx = jnp.arange(B_GROUPS)[None, :, None]

    def one_block(n):
        t = n * Q_BLOCK + jnp.arange(Q_BLOCK)
        pos = t[:, None, None] + offs[None, :, :]
        valid = (pos >= 0) & (pos < seq_len)
        pos_c = jnp.clip(pos, 0, seq_len - 1)
        qblk = lax.dynamic_slice_in_dim(q, n * Q_BLOCK, Q_BLOCK, axis=1)
        ks = k[:, pos_c, g_idx]
        vs = v[:, pos_c, g_idx]
        s = jnp.einsum('bqghd,bqgjhd->bqghj', qblk, ks, preferred_element_type=jnp.float32) + bias
        s = jnp.where(valid[None, :, :, None, :], s, -jnp.inf)
        lse = jax.nn.logsumexp(s, axis=-1, keepdims=True)
        p = jnp.exp(s - lse)
        o = jnp.einsum('bqghj,bqgjhd->bqghd', p.astype(v.dtype), vs)
        w = jax.nn.softmax(lse[..., 0], axis=2)
        return jnp.einsum('bqgh,bqghd->bqhd', w.astype(o.dtype), o)

    o = lax.map(one_block, jnp.arange(nb))
    return o.swapaxes(0, 1).reshape(bsz, seq_len, B_OUT_W)


def _hybrid_layer(x, w_in, b_gate, q_norm_a, k_norm_a, w_proj_a, w_proj_b, w_out,
                  ln1_g, ln1_b, w_ffn_gate, w_ffn_up, w_ffn_down, ln2_g, ln2_b, rope_tabs):
    bsz, seq_len, _ = x.shape
    scale = HEAD_DIM ** -0.5
    split_pts = [int(p) for p in np.cumsum(IN_WIDTHS)[:-1]]
    qa, ka, va, qb, kb, vb, ga, gb = jnp.split(x @ w_in, split_pts, axis=-1)
    qa = _apply_axial_rope(_rms_norm(qa.reshape(bsz, seq_len, A_Q_HEADS, HEAD_DIM), q_norm_a), rope_tabs)
    ka = _apply_axial_rope(_rms_norm(ka.reshape(bsz, seq_len, A_KV_HEADS, HEAD_DIM), k_norm_a), rope_tabs)
    va = va.reshape(bsz, seq_len, A_KV_HEADS, HEAD_DIM)
    out_a = _gqa_blocks(qa * scale, ka, va)
    shp = (bsz, seq_len, B_GROUPS, B_HEADS_PER_GROUP, HEAD_DIM)
    out_b = _dilated_window_attention(qb.reshape(shp) * scale, kb.reshape(shp), vb.reshape(shp))
    gate_a = jax.nn.sigmoid(ga + b_gate[0])
    gate_b = jax.nn.sigmoid(gb + b_gate[1])
    mixed = (gate_a * (out_a @ w_proj_a) + gate_b * (out_b @ w_proj_b)) @ w_out
    x = _layer_norm(ALPHA * x + mixed, ln1_g, ln1_b)
    h = jax.nn.silu(x @ w_ffn_gate) * (x @ w_ffn_up)
    x = _layer_norm(ALPHA * x + h @ w_ffn_down, ln2_g, ln2_b)
    return x


def setup_inputs(seed: int = 0) -> dict:
    key = jax.random.key(seed)
    ks = jax.random.split(key, 16)
    f32 = jnp.float32

    def nrm(k, shape, fan_in, s=1.0):
        return jax.random.normal(k, shape, f32) * (s * fan_in ** -0.5)

    col_scale = jnp.concatenate([jnp.full((w,), sc, f32) for w, sc in
                                 zip(IN_WIDTHS, (1.0, 1.0, BETA, 1.0, 1.0, BETA, 1.0, 1.0))])
    return {
        'x': jax.random.normal(ks[0], (BATCH, SEQ, D_MODEL), f32),
        'w_in': nrm(ks[1], (DEPTH, D_MODEL, IN_TOTAL), D_MODEL) * col_scale,
        'b_gate': 0.02 * jax.random.normal(ks[2], (DEPTH, 2, D_MODEL), f32),
        'q_norm_a': 1.0 + 0.02 * jax.random.normal(ks[3], (DEPTH, HEAD_DIM), f32),
        'k_norm_a': 1.0 + 0.02 * jax.random.normal(ks[4], (DEPTH, HEAD_DIM), f32),
        'w_proj_a': nrm(ks[5], (DEPTH, A_OUT_W, D_MODEL), A_OUT_W, BETA),
        'w_proj_b': nrm(ks[6], (DEPTH, B_OUT_W, D_MODEL), B_OUT_W, BETA),
        'w_out': nrm(ks[7], (DEPTH, D_MODEL, D_MODEL), D_MODEL, BETA),
        'ln1_g': 1.0 + 0.02 * jax.random.normal(ks[8], (DEPTH, D_MODEL), f32),
        'ln1_b': 0.02 * jax.random.normal(ks[9], (DEPTH, D_MODEL), f32),
        'w_ffn_gate': nrm(ks[10], (DEPTH, D_MODEL, FFN_HIDDEN), D_MODEL),
        'w_ffn_up': nrm(ks[11], (DEPTH, D_MODEL, FFN_HIDDEN), D_MODEL),
        'w_ffn_down': nrm(ks[12], (DEPTH, FFN_HIDDEN, D_MODEL), FFN_HIDDEN, BETA),
        'ln2_g': 1.0 + 0.02 * jax.random.normal(ks[13], (DEPTH, D_MODEL), f32),
        'ln2_b': 0.02 * jax.random.normal(ks[14], (DEPTH, D_MODEL), f32),
    }


def reference(x, w_in, b_gate, q_norm_a, k_norm_a, w_proj_a, w_proj_b, w_out,
              ln1_g, ln1_b, w_ffn_gate, w_ffn_up, w_ffn_down, ln2_g, ln2_b):
    rope_tabs = _axial_rope_tables(x.shape[1])
    for l in range(DEPTH):
        x = _hybrid_layer(x, w_in[l], b_gate[l], q_norm_a[l], k_norm_a[l], w_proj_a[l], w_proj_b[l],
                          w_out[l], ln1_g[l], ln1_b[l], w_ffn_gate[l], w_ffn_up[l], w_ffn_down[l],
                          ln2_g[l], ln2_b[l], rope_tabs)
    return x
```

```python
import numpy as np
import concourse.bass as bass
import concourse.mybir as mybir
from concourse.bass_utils import run_bass_kernel_spmd

F32 = mybir.dt.float32
BF16 = mybir.dt.bfloat16
AF = mybir.ActivationFunctionType
ALU = mybir.AluOpType

D = 2048
S = 4096
T = 2048
HID = 5632
NHC = HID // 128
ALPHA = 2.0 ** 0.25
SCALE = 128.0 ** -0.5
RMS_EPS = 1e-6
LN_EPS = 1e-5
B_DIL = (1, 4, 16)
B_LEN = (2176, 2560, 4096)
B_M = (17, 5, 2)
B_NQ = (16, 4, 1)

ENGS = ("pe", "act", "dve", "pool", "sp")


class Op:
    __slots__ = ("eng", "fn", "dma", "deps", "signal", "tick", "slot", "expect",
                 "waits", "qidx", "seq")

    def __init__(self, eng, fn, dma):
        self.eng = eng
        self.fn = fn
        self.dma = dma
        self.deps = set()
        self.signal = False
        self.tick = 0
        self.slot = 0
        self.expect = 0
        self.waits = []
        self.qidx = 0


class Sched:
    def __init__(self, nc):
        self.nc = nc
        self.ops = []
        self.last_writer = {}
        self.readers = {}
        self.n_dma = {"sp": 16, "pool": 8, "act": 4}
        self.fence_pending = {}
        self.last_op = {}
        self.dma_all = []

    def add(self, eng, fn, reads=(), writes=(), dma=False):
        op = Op(eng, fn, dma)
        op.seq = len(self.ops)
        deps = set()
        for k in reads:
            w = self.last_writer.get(k)
            if w is not None:
                deps.add(w)
            self.readers.setdefault(k, []).append(op)
        for k in writes:
            w = self.last_writer.get(k)
            if w is not None:
                deps.add(w)
            for r in self.readers.get(k, ()):
                if r is not op:
                    deps.add(r)
        for k in writes:
            self.last_writer[k] = op
            self.readers[k] = []
        fp = self.fence_pending.get(eng)
        if fp:
            deps |= fp
            self.fence_pending[eng] = None
        op.deps = deps
        self.ops.append(op)
        if dma:
            self.dma_all.append(op)
        else:
            self.last_op[eng] = op
        return op

    def fence(self):
        s = set(self.last_op.values()) | set(self.dma_all)
        self.dma_all = []
        for e in ENGS:
            cur = self.fence_pending.get(e)
            self.fence_pending[e] = (cur | s) if cur else set(s)
        self.last_writer = {}
        self.readers = {}

    def pe(self, fn, reads=(), writes=()):
        return self.add("pe", fn, reads, writes)

    def act(self, fn, reads=(), writes=()):
        return self.add("act", fn, reads, writes)

    def dve(self, fn, reads=(), writes=()):
        return self.add("dve", fn, reads, writes)

    def pool(self, fn, reads=(), writes=()):
        return self.add("pool", fn, reads, writes)

    def dma(self, q, fn, reads=(), writes=()):
        return self.add(q, fn, reads, writes, dma=True)

    def emit(self):
        nc = self.nc
        ops = self.ops

        def skip(d, op):
            return d.eng == "pe" and op.eng == "pe" and not d.dma and not op.dma

        for op in ops:
            latest = {}
            keep = set()
            for d in op.deps:
                if skip(d, op):
                    continue
                if d.dma:
                    keep.add(d)
                else:
                    cur = latest.get(d.eng)
                    if cur is None or d.seq > cur.seq:
                        latest[d.eng] = d
            keep |= set(latest.values())
            op.deps = keep
            for d in keep:
                d.signal = True
        cnt = {e: 0 for e in ENGS}
        dcnt = {e: 0 for e in ENGS}
        for op in ops:
            if op.dma:
                i = dcnt[op.eng]
                dcnt[op.eng] += 1
                n = self.n_dma[op.eng]
                op.qidx = i
                op.slot = i % n
                op.expect = 16 * (i // n + 1)
                op.signal = True
            elif op.signal:
                cnt[op.eng] += 1
                op.tick = cnt[op.eng]
        esem = {e: nc.alloc_semaphore("es_" + e) for e in ENGS}
        dsem = {q: [nc.alloc_semaphore("ds_%s%d" % (q, i)) for i in range(self.n_dma[q])]
                for q in self.n_dma if dcnt[q] > 0}
        known = {e: {} for e in ENGS}
        per_eng = {e: [] for e in ENGS}
        for op in ops:
            need = {}
            for d in op.deps:
                if skip(d, op):
                    continue
                if d.dma:
                    key = ("d", d.eng, d.slot)
                    val = d.expect
                else:
                    key = ("e", d.eng)
                    val = d.tick
                if need.get(key, 0) < val:
                    need[key] = val
            if op.dma and op.qidx >= self.n_dma[op.eng]:
                key = ("d", op.eng, op.slot)
                val = op.expect - 16
                if need.get(key, 0) < val:
                    need[key] = val
            kn = known[op.eng]
            w = []
            for key, val in need.items():
                if kn.get(key, 0) >= val:
                    continue
                kn[key] = val
                sem = esem[key[1]] if key[0] == "e" else dsem[key[1]][key[2]]
                w.append((sem, val))
            op.waits = w
            per_eng[op.eng].append(op)
        final_waits = []
        for q in dsem:
            n = self.n_dma[q]
            tot = dcnt[q]
            for sl in range(n):
                uses = (tot - sl + n - 1) // n if tot > sl else 0
                if uses > 0:
                    final_waits.append((dsem[q][sl], 16 * uses))
        for e in ENGS:
            if cnt[e] > 0:
                final_waits.append((esem[e], cnt[e]))
        self.stats = {"n_ops": len(ops), "cnt": cnt, "dcnt": dcnt,
                      "n_waits": sum(len(o.waits) for o in ops),
                      "per_eng": {e: len(per_eng[e]) for e in ENGS}}
        handles = {"pe": "tensor", "act": "scalar", "dve": "vector",
                   "pool": "gpsimd", "sp": "sync"}
        with nc.Block() as block:
            for e in ENGS:
                lst = per_eng[e]
                is_sp = (e == "sp")
                if not lst and not is_sp:
                    continue

                def body(eng, lst=lst, e=e, is_sp=is_sp):
                    for op in lst:
                        for (sem, val) in op.waits:
                            eng.wait_ge(sem, val)
                        ins = op.fn(eng)
                        if op.signal:
                            if op.dma:
                                ins.then_inc(dsem[op.eng][op.slot], 16)
                            else:
                                ins.then_inc(esem[e], 1)
                    if is_sp:
                        for (sem, val) in final_waits:
                            eng.wait_ge(sem, val)

                getattr(block, handles[e])(body)


class Arena:
    def __init__(self, nc):
        self.nc = nc
        self.lo = (nc.sbuf_base + 63) // 64 * 64
        self.hi = nc.sbuf_top
        self.cur = self.lo
        self.n = 0

    def alloc(self, name, shape, dtype):
        esz = 2 if dtype == BF16 else 4
        nbytes = esz
        for d in shape[1:]:
            nbytes *= d
        nbytes = (nbytes + 63) // 64 * 64
        assert self.cur + nbytes <= self.hi, ("SBUF overflow", name, self.cur, nbytes, self.hi)
        self.n += 1
        t = self.nc.alloc_sbuf_tensor_at("%s_%d" % (name, self.n), list(shape), dtype, offset=self.cur)
        self.cur += nbytes
        return t

    def mark(self):
        return self.cur

    def reset(self, m):
        self.cur = m


def MM(out, lhsT, rhs, start=True, stop=True):
    return lambda e: e.matmul(out, lhsT=lhsT, rhs=rhs, start=start, stop=stop)


def TR(out, in_, ident):
    return lambda e: e.transpose(out, in_, ident)


def DMA(out, in_):
    return lambda e: e.dma_start(out=out, in_=in_)


def ACTF(out, in_, func, **kw):
    return lambda e: e.activation(out=out, in_=in_, func=func, **kw)


def TT(out, in0, in1, op):
    return lambda e: e.tensor_tensor(out=out, in0=in0, in1=in1, op=op)


def TS(out, in0, s1, op0, s2=None, op1=None):
    if op1 is None:
        return lambda e: e.tensor_scalar(out=out, in0=in0, scalar1=s1, scalar2=None, op0=op0)
    return lambda e: e.tensor_scalar(out=out, in0=in0, scalar1=s1, scalar2=s2, op0=op0, op1=op1)


def STT(out, in0, scalar, in1, op0, op1):
    return lambda e: e.scalar_tensor_tensor(out=out, in0=in0, scalar=scalar, in1=in1, op0=op0, op1=op1)


def CP(out, in_):
    return lambda e: e.tensor_copy(out=out, in_=in_)


def ACP(out, in_):
    return lambda e: e.copy(out=out, in_=in_)


def RECIP(out, in_):
    return lambda e: e.reciprocal(out=out, in_=in_)


def MEMSET(ap, v):
    return lambda e: e.memset(ap, v)


def BNS(out, in_):
    return lambda e: e.bn_stats(out=out, in_=in_)


def BNA(out, in_):
    return lambda e: e.bn_aggr(out=out, in_=in_)


def build_program(stop_after=99):
    nc = bass.Bass("TRN2", target_bir_lowering=False)
    dbg = stop_after < 99

    def din(name, shape, dt=F32):
        return nc.dram_tensor(name, list(shape), dt, kind="ExternalInput").ap()

    def dscr(name, shape, dt, tap=()):
        kind = "ExternalOutput" if (stop_after in tap) else "Internal"
        return nc.dram_tensor(name, list(shape), dt, kind=kind).ap()

    xs = din("xs", [S, D])
    w_in = din("w_in", [D, 10240])
    w_pa = din("w_proj_a", [1024, D])
    w_pb = din("w_proj_b", [512, D])
    w_out = din("w_out", [D, D])
    w_fg = din("w_ffn_gate", [D, HID])
    w_fu = din("w_ffn_up", [D, HID])
    w_fd = din("w_ffn_down", [HID, D])
    bgate_d = din("bgate_t", [128, 32])
    qkg_d = din("qkg", [128, 2])
    ln_d = din("ln_t", [4, 128, D])
    ident_d = din("ident", [128, 128])
    rotm_d = din("rotm", [128, 128])
    ropeC = din("ropeC", [128, S])
    ropeS = din("ropeS", [128, S])
    bbias_d = din("bbias", [4, 128, 1152])
    out_d = nc.dram_tensor("out", [T, D], F32, kind="ExternalOutput").ap()

    qaT_d = dscr("qaT", [8, 128, T], BF16, tap=(2,))
    kaT_d = dscr("kaT", [2, 128, S], BF16, tap=(2,))
    va_d = dscr("va", [2, 128, 4096], BF16, tap=(2,))
    qbT_d = dscr("qbT", [12, 128, T], BF16, tap=(2,))
    kbT_d = dscr("kbT", [12, 128, S], BF16, tap=(2,))
    vb_d = dscr("vb", [12, 128, 4096], BF16, tap=(2,))
    sga_d = dscr("sga", [16, 128, T], F32, tap=(2,))
    sgb_d = dscr("sgb", [16, 128, T], F32, tap=(2,))
    x1_d = dscr("x1", [T, D], F32, tap=(6, 8))
    x1T_d = dscr("x1T", [4, 128, 8192], BF16, tap=(6,))
    wgb_d = dscr("wgb", [22, 128, 4096], BF16)
    wub_d = dscr("wub", [22, 128, 4096], BF16)
    wdb_d = dscr("wdb", [HID, D], BF16)
    dbg_oa = dscr("dbg_oa", [8, 128, T], BF16, tap=(3, 4, 8))
    dbg_ob = dscr("dbg_ob", [4, 128, T], BF16, tap=(4, 8))
    dbg_mix = dscr("dbg_mix", [16, 128, T], BF16, tap=(5, 8))
    dbg_y = dscr("dbg_y", [T, D], F32, tap=(6,))

    s = Sched(nc)
    ar = Arena(nc)
    pbig = [nc.alloc_psum_tensor("pbig%d" % i, [128, 1024], F32) for i in range(4)]
    psb = [pbig[i // 2][:, (i % 2) * 512:(i % 2 + 1) * 512] for i in range(8)]

    def PSK(i):
        return ("ps", i)

    ident_b = ar.alloc("ident_b", [128, 128], BF16)
    ones_b = ar.alloc("ones_b", [128, 128], BF16)
    onesf = ar.alloc("onesf", [128, 128], F32)
    rotm = ar.alloc("rotm", [128, 128], F32)
    bgate = ar.alloc("bgate", [128, 32], F32)
    qkg = ar.alloc("qkg", [128, 2], F32)
    epsr = ar.alloc("epsr", [128, 1], F32)
    epsl = ar.alloc("epsl", [128, 1], F32)
    s.dma("pool", DMA(ident_b[:, :], ident_d[:, :]), writes=["ident_b"])
    s.dma("sp", DMA(rotm[:, :], rotm_d[:, :]), writes=["rotm"])
    s.dma("sp", DMA(bgate[:, :], bgate_d[:, :]), writes=["bgate"])
    s.dma("sp", DMA(qkg[:, :], qkg_d[:, :]), writes=["qkg"])
    s.pool(MEMSET(ones_b[:, :], 1.0), writes=["ones_b"])
    s.pool(MEMSET(onesf[:, :], 1.0 / 128.0), writes=["onesf"])
    s.pool(MEMSET(epsr[:, :], RMS_EPS), writes=["epsr"])
    s.pool(MEMSET(epsl[:, :], LN_EPS), writes=["epsl"])
    s.fence()
    base_mark = ar.mark()

    precast = []
    for j in range(22):
        for q in range(4):
            precast.append((wgb_d[j].rearrange("p (kc c) -> p kc c", c=256)[:, q * 4:(q + 1) * 4, :],
                            w_fg[:, j * 256:(j + 1) * 256].rearrange("(kc p) c -> p kc c", p=128)[:, q * 4:(q + 1) * 4, :], None))
            precast.append((wub_d[j].rearrange("p (kc c) -> p kc c", c=256)[:, q * 4:(q + 1) * 4, :],
                            w_fu[:, j * 256:(j + 1) * 256].rearrange("(kc p) c -> p kc c", p=128)[:, q * 4:(q + 1) * 4, :], None))
    for hc in range(0, NHC, 2):
        precast.append((wdb_d[hc * 128:(hc + 2) * 128, :], w_fd[hc * 128:(hc + 2) * 128, :], ("wdb", hc // 2)))
    precast_iter = iter(precast)
    precast_ops = []

    def issue_precast(n=1):
        for _ in range(n):
            job = next(precast_iter, None)
            if job is None:
                return
            op = s.dma("pool", DMA(job[0], job[1]))
            s.dma_all.remove(op)
            precast_ops.append(op)

    xT = ar.alloc("xT", [128, 16, S], BF16)
    p1_mark = ar.mark()
    xin32 = [ar.alloc("xin32", [128, D], F32) for _ in range(3)]
    xin16 = [ar.alloc("xin16", [128, D], BF16) for _ in range(2)]
    for tt in range(32):
        b3 = tt % 3
        b2 = tt % 2
        s.dma("sp", DMA(xin32[b3][:, :], xs[tt * 128:(tt + 1) * 128, :]), writes=[("x32", b3)])
        if tt % 2 == 0:
            s.dve(CP(xin16[b2][:, :], xin32[b3][:, :]), reads=[("x32", b3)], writes=[("x16", b2)])
        else:
            s.pool(CP(xin16[b2][:, :], xin32[b3][:, :]), reads=[("x32", b3)], writes=[("x16", b2)])
        for k in range(16):
            bank = (tt % 2) * 4 + k // 4
            col = (k % 4) * 128
            s.pe(MM(psb[bank][:, col:col + 128], xin16[b2][:, k * 128:(k + 1) * 128], ident_b[:, :]),
                 reads=[("x16", b2)], writes=[PSK(bank)])
        for q in range(4):
            bank = (tt % 2) * 4 + q
            src = psb[bank][:, :].rearrange("p (k t) -> p k t", t=128)
            dst = xT[:, q * 4:(q + 1) * 4, tt * 128:(tt + 1) * 128]
            if q % 2 == 0:
                s.act(ACP(dst, src), reads=[PSK(bank)], writes=[("xT", tt, q)])
            else:
                s.dve(CP(dst, src), reads=[PSK(bank)], writes=[("xT", tt, q)])
    s.fence()
    ar.reset(p1_mark)
    if stop_after <= 1:
        xT_dbg = nc.dram_tensor("dbg_xT", [128, 16, S], BF16, kind="ExternalOutput").ap()
        s.dma("sp", DMA(xT_dbg[:, :, :], xT[:, :, :]))
        s.emit()
        return nc, s

    wbuf = [ar.alloc("wbuf", [128, 16, 256], BF16) for _ in range(3)]
    f32t = {n: [ar.alloc(n, [128, 512], F32) for _ in range(2)] for n in ("sq", "y", "r", "t1", "t2", "cc", "ss")}
    stg = [ar.alloc("stg", [128, 512], BF16) for _ in range(4)]
    gstg = [ar.alloc("gstg", [128, 512], F32) for _ in range(3)]
    vstg = [ar.alloc("vstg", [128, 256], BF16) for _ in range(4)]
    cnt = {"stg": 0, "gstg": 0, "vstg": 0, "acc": 0, "rr": 0, "ev": 0}

    def load_wblock(j):
        b = j % 3
        src = w_in[:, j * 256:(j + 1) * 256].rearrange("(kc p) c -> p kc c", p=128)
        for q in range(4):
            s.dma("pool", DMA(wbuf[b][:, q * 4:(q + 1) * 4, :], src[:, q * 4:(q + 1) * 4, :]), writes=[("wbuf", b, q)])

    def fm_accum(j, cc, tb):
        b = j % 3
        bank = cnt["acc"] % 4
        cnt["acc"] += 1
        for kc in range(16):
            s.pe(MM(psb[bank][:, :], wbuf[b][:, kc, cc * 128:(cc + 1) * 128], xT[:, kc, tb * 512:(tb + 1) * 512],
                    start=(kc == 0), stop=(kc == 15)),
                 reads=[("wbuf", b, kc // 4)], writes=[PSK(bank)])
        return bank

    rr_pending = []

    def rms_rope(bank, tb, gcol, scale, dst_dram):
        i = cnt["rr"] % 2
        cnt["rr"] += 1
        sq, y, r, t1, t2, cc_, ss_ = (f32t[n][i] for n in ("sq", "y", "r", "t1", "t2", "cc", "ss"))
        K = lambda n: (n, i)
        s.dma("sp", DMA(cc_[:, :], ropeC[:, tb * 512:(tb + 1) * 512]), writes=[K("cc")])
        s.dma("sp", DMA(ss_[:, :], ropeS[:, tb * 512:(tb + 1) * 512]), writes=[K("ss")])
        s.act(ACTF(sq[:, :], psb[bank][:, :], AF.Square), reads=[PSK(bank)], writes=[K("sq")])
        s.act(ACTF(y[:, :], psb[bank][:, :], AF.Copy, scale=qkg[:, gcol:gcol + 1]), reads=[PSK(bank)], writes=[K("y")])

        def stage_b():
            pa = 4 + i
            pbk = 6 + i
            s.pe(MM(psb[pa][:, :], onesf[:, :], sq[:, :]), reads=[K("sq")], writes=[PSK(pa)])
            s.pe(MM(psb[pbk][:, :], rotm[:, :], y[:, :]), reads=[K("y")], writes=[PSK(pbk)])
            s.act(ACTF(r[:, :], psb[pa][:, :], AF.Ln, bias=epsr[:, 0:1], scale=1.0), reads=[PSK(pa)], writes=[K("r")])
            s.act(ACTF(r[:, :], r[:, :], AF.Exp, scale=-0.5), reads=[K("r")], writes=[K("r")])
            s.pool(TT(t1[:, :], y[:, :], cc_[:, :], ALU.mult), reads=[K("y"), K("cc")], writes=[K("t1")])
            s.dve(TT(t2[:, :], psb[pbk][:, :], ss_[:, :], ALU.mult), reads=[PSK(pbk), K("ss")], writes=[K("t2")])
            s.pool(TT(t1[:, :], t1[:, :], t2[:, :], ALU.add), reads=[K("t1"), K("t2")], writes=[K("t1")])
            si = cnt["stg"] % 4
            cnt["stg"] += 1
            s.dve(STT(stg[si][:, :], t1[:, :], scale, r[:, :], ALU.mult, ALU.mult),
                  reads=[K("t1"), K("r")], writes=[("stg", si)])
            s.dma("sp", DMA(dst_dram, stg[si][:, :]), reads=[("stg", si)])

        rr_pending.append(stage_b)
        if len(rr_pending) > 1:
            rr_pending.pop(0)()

    def rr_flush():
        while rr_pending:
            rr_pending.pop(0)()

    def plain_evac(bank, scale, dst_dram):
        si = cnt["stg"] % 4
        cnt["stg"] += 1
        use_act = (cnt["ev"] % 2 == 0) or scale != 1.0
        cnt["ev"] += 1
        if use_act:
            s.act(ACTF(stg[si][:, :], psb[bank][:, :], AF.Copy, scale=scale), reads=[PSK(bank)], writes=[("stg", si)])
        else:
            s.dve(CP(stg[si][:, :], psb[bank][:, :]), reads=[PSK(bank)], writes=[("stg", si)])
        s.dma("sp", DMA(dst_dram, stg[si][:, :]), reads=[("stg", si)])

    def gate_evac(bank, bcol, dst_dram):
        gi = cnt["gstg"] % 3
        cnt["gstg"] += 1
        s.act(ACTF(gstg[gi][:, :], psb[bank][:, :], AF.Sigmoid, bias=bgate[:, bcol:bcol + 1], scale=1.0),
              reads=[PSK(bank)], writes=[("gstg", gi)])
        s.dma("sp", DMA(dst_dram, gstg[gi][:, :]), reads=[("gstg", gi)])

    def tm_block(j, dil, M, dst, h0):
        b = j % 3
        for r in range(dil):
            for m in range(M):
                bank = cnt["acc"] % 4
                cnt["acc"] += 1
                t0 = r + dil * 128 * m
                for kc in range(16):
                    s.pe(MM(psb[bank][:, 0:256], xT[:, kc, t0:t0 + dil * 127 + 1:dil], wbuf[b][:, kc, :],
                            start=(kc == 0), stop=(kc == 15)),
                         reads=[("wbuf", b, kc // 4)], writes=[PSK(bank)])
                vi = cnt["vstg"] % 4
                cnt["vstg"] += 1
                s.dve(CP(vstg[vi][:, :], psb[bank][:, 0:256]), reads=[PSK(bank)], writes=[("vstg", vi)])
                for a in range(2):
                    s.dma("sp", DMA(dst[h0 + a, :, (r * M + m) * 128:(r * M + m + 1) * 128], vstg[vi][:, a * 128:(a + 1) * 128]),
                          reads=[("vstg", vi)])

    NBLK = 40
    load_wblock(0)
    load_wblock(1)
    for j in range(NBLK):
        if j + 2 < NBLK:
            load_wblock(j + 2)
        issue_precast(3)
        c0 = j * 256
        if c0 < 1024:
            for cc in range(2):
                h = (c0 // 128) + cc
                for tb in range(4):
                    bank = fm_accum(j, cc, tb)
                    rms_rope(bank, tb, 0, SCALE, qaT_d[h, :, tb * 512:(tb + 1) * 512])
        elif c0 < 1280:
            for cc in range(2):
                for tb in range(8):
                    bank = fm_accum(j, cc, tb)
                    rms_rope(bank, tb, 1, 1.0, kaT_d[cc, :, tb * 512:(tb + 1) * 512])
        elif c0 < 1536:
            rr_flush()
            tm_block(j, 1, 32, va_d, 0)
        elif c0 < 3072:
            for cc in range(2):
                h = (c0 - 1536) // 128 + cc
                for tb in range(4):
                    bank = fm_accum(j, cc, tb)
                    plain_evac(bank, SCALE, qbT_d[h, :, tb * 512:(tb + 1) * 512])
        elif c0 < 4608:
            for cc in range(2):
                h = (c0 - 3072) // 128 + cc
                ntb = 8 if h >= 8 else 5
                for tb in range(ntb):
                    bank = fm_accum(j, cc, tb)
                    plain_evac(bank, 1.0, kbT_d[h, :, tb * 512:(tb + 1) * 512])
        elif c0 < 6144:
            h0 = (c0 - 4608) // 128
            g = h0 // 4
            tm_block(j, B_DIL[g], B_M[g], vb_d, h0)
        else:
            which = 0 if c0 < 8192 else 1
            dst = sga_d if which == 0 else sgb_d
            for cc in range(2):
                f = ((c0 - 6144) % 2048) // 128 + cc
                for tb in range(4):
                    bank = fm_accum(j, cc, tb)
                    gate_evac(bank, which * 16 + f, dst[f, :, tb * 512:(tb + 1) * 512])
    s.fence()
    ar.reset(base_mark)
    if stop_after <= 2:
        s.emit()
        return nc, s

    mixT = ar.alloc("mixT", [128, 16, T], BF16)
    p6_mark = ar.mark()
    outaT = ar.alloc("outaT", [128, 8, T], BF16)
    outbT = ar.alloc("outbT", [128, 4, T], BF16)
    p3_mark = ar.mark()
    KT = [ar.alloc("KT", [128, S], BF16) for _ in range(2)]
    VV = [ar.alloc("VV", [128, 4096], BF16) for _ in range(2)]
    QT = [ar.alloc("QT", [128, T], BF16) for _ in range(2)]
    PT2 = [ar.alloc("PT2", [128, 1024], BF16) for _ in range(4)]
    PS = [ar.alloc("PS", [128, 512], BF16) for _ in range(4)]
    RD = [ar.alloc("RD", [128, 512], F32) for _ in range(2)]

    def load_k(g):
        for q in range(4):
            s.dma("sp", DMA(KT[g][:, q * 1024:(q + 1) * 1024], kaT_d[g][:, q * 1024:(q + 1) * 1024]), writes=[("KT", g, q)])

    def load_v(g):
        for q in range(4):
            s.dma("sp", DMA(VV[g][:, q * 1024:(q + 1) * 1024], va_d[g][:, q * 1024:(q + 1) * 1024]), writes=[("VV", g, q)])

    def load_q(h):
        s.dma("sp", DMA(QT[h % 2][:, :], qaT_d[h]), writes=[("QT", h % 2)])

    load_q(0)
    load_k(0)
    load_v(0)
    load_k(1)
    load_v(1)
    blk = 0
    pcn = 0
    LAG3 = 2
    for h in range(8):
        g = h // 4
        if h + 1 < 8:
            load_q(h + 1)
        qt = QT[h % 2]
        for qb in range(4):
            issue_precast(3)
            po = 4 + (blk % 2)
            pd = 6 + (blk % 2)
            blk += 1
            bufs = {}
            for pc in range(16 + LAG3):
                if pc < 16:
                    bk = pcn % 2
                    pi = pcn % 4
                    pcn += 1
                    bufs[pc] = pi
                    for e2 in range(2):
                        c = 2 * pc + e2
                        s.pe(MM(pbig[bk][:, e2 * 512:(e2 + 1) * 512], KT[g][:, c * 128:(c + 1) * 128],
                                qt[:, qb * 512:(qb + 1) * 512]),
                             reads=[("KT", g, c // 8), ("QT", h % 2)], writes=[("pbig", bk)])
                    s.act(ACTF(PT2[pi][:, :], pbig[bk][:, :], AF.Exp), reads=[("pbig", bk)], writes=[("PT2", pi)])
                    s.dve(TT(PS[pi][:, :], PT2[pi][:, 0:512], PT2[pi][:, 512:1024], ALU.add),
                          reads=[("PT2", pi)], writes=[("PS", pi)])
                if pc >= LAG3:
                    pp = pc - LAG3
                    pi = bufs[pp]
                    for e2 in range(2):
                        cc = 2 * pp + e2
                        s.pe(MM(psb[po][:, :], VV[g][:, cc * 128:(cc + 1) * 128], PT2[pi][:, e2 * 512:(e2 + 1) * 512],
                                start=(cc == 0), stop=(cc == 31)),
                             reads=[("VV", g, cc // 8), ("PT2", pi)], writes=[PSK(po)])
                    s.pe(MM(psb[pd][:, :], ones_b[:, :], PS[pi][:, :], start=(pp == 0), stop=(pp == 15)),
                         reads=[("PS", pi)], writes=[PSK(pd)])
            ri = blk % 2
            s.dve(RECIP(RD[ri][:, :], psb[pd][:, :]), reads=[PSK(pd)], writes=[("RD", ri)])
            s.dve(TT(outaT[:, h, qb * 512:(qb + 1) * 512], psb[po][:, :], RD[ri][:, :], ALU.mult),
                  reads=[PSK(po), ("RD", ri)], writes=[("outaT", h, qb)])
    s.fence()
    ar.reset(p3_mark)
    if stop_after in (3, 4, 8):
        for h_ in range(8):
            s.dma("sp", DMA(dbg_oa[h_], outaT[:, h_, :]))
    if stop_after <= 3:
        s.emit()
        return nc, s

    QB = [ar.alloc("QB", [128, T], BF16) for _ in range(2)]
    KB = [ar.alloc("KB", [128, S], BF16) for _ in range(2)]
    VB = [ar.alloc("VB", [128, 4096], BF16) for _ in range(2)]
    BIAS = [ar.alloc("BIAS", [128, 1152], F32) for _ in range(2)]
    SB = [ar.alloc("SB", [128, 384], F32) for _ in range(4)]
    PB = [ar.alloc("PB", [128, 384], BF16) for _ in range(4)]
    ACC = ar.alloc("ACC", [128, 2, T], F32)
    RB = ar.alloc("RB", [128, T], F32)
    OST = [ar.alloc("OST", [128, 256], F32) for _ in range(4)]
    combos = [(hh, g) for hh in range(4) for g in range(3)]

    def load_b(ci):
        hh, g = combos[ci]
        hd = g * 4 + hh
        b = ci % 2
        dil, L, M = B_DIL[g], B_LEN[g], B_M[g]
        s.dma("sp", DMA(QB[b][:, :], qbT_d[hd]), writes=[("QB", b)])
        s.dma("sp", DMA(KB[b][:, 0:L], kbT_d[hd, :, 0:L]), writes=[("KB", b)])
        s.dma("sp", DMA(VB[b][:, 0:dil * M * 128], vb_d[hd, :, 0:dil * M * 128]), writes=[("VB", b)])
        if g == 0:
            s.dma("sp", DMA(BIAS[hh % 2][:, :], bbias_d[hh]), writes=[("BIAS", hh % 2)])

    qsets = []
    for ci, (hh, g) in enumerate(combos):
        dil, M, NQ = B_DIL[g], B_M[g], B_NQ[g]
        for r in range(dil):
            for n in range(NQ):
                qsets.append((ci, hh, g, r, n, (r == 0 and n == 0), (r == dil - 1 and n == NQ - 1)))

    def b_stage1(t):
        ci, hh, g, r, n, first, last = qsets[t]
        b = ci % 2
        dil, M = B_DIL[g], B_M[g]
        ms = [m for m in (n - 1, n, n + 1) if 0 <= m < M]
        ps_s = t % 4
        sbi = t % 4
        q0 = r + dil * 128 * n
        qap = QB[b][:, q0:q0 + dil * 127 + 1:dil]
        for m in ms:
            col = (m - n + 1) * 128
            k0 = r + dil * 128 * m
            kap = KB[b][:, k0:k0 + dil * 127 + 1:dil]
            s.pe(MM(psb[ps_s][:, col:col + 128], kap, qap), reads=[("KB", b), ("QB", b)], writes=[PSK(ps_s)])
        c_lo = (ms[0] - n + 1) * 128
        c_hi = (ms[-1] - n + 2) * 128
        s.dve(TT(SB[sbi][:, c_lo:c_hi], psb[ps_s][:, c_lo:c_hi],
                 BIAS[hh % 2][:, g * 384 + c_lo:g * 384 + c_hi], ALU.add),
              reads=[PSK(ps_s), ("BIAS", hh % 2)], writes=[("SB", sbi)])
        s.act(ACTF(PB[sbi][:, c_lo:c_hi], SB[sbi][:, c_lo:c_hi], AF.Exp), reads=[("SB", sbi)], writes=[("PB", sbi)])

    def b_stage2(t):
        ci, hh, g, r, n, first, last = qsets[t]
        if first and ci + 1 < len(combos):
            load_b(ci + 1)
        b = ci % 2
        dil, M = B_DIL[g], B_M[g]
        ms = [m for m in (n - 1, n, n + 1) if 0 <= m < M]
        sbi = t % 4
        ps_o = 4 + (t % 4)
        q0 = r + dil * 128 * n
        for idx, m in enumerate(ms):
            col = (m - n + 1) * 128
            s.pe(MM(psb[ps_o][:, 0:128], VB[b][:, (r * M + m) * 128:(r * M + m + 1) * 128], PB[sbi][:, col:col + 128],
                    start=(idx == 0), stop=(idx == len(ms) - 1)),
                 reads=[("VB", b), ("PB", sbi)], writes=[PSK(ps_o)])
        for idx, m in enumerate(ms):
            col = (m - n + 1) * 128
            s.pe(MM(psb[ps_o][:, 128:256], ones_b[:, :], PB[sbi][:, col:col + 128],
                    start=(idx == 0), stop=(idx == len(ms) - 1)),
                 reads=[("PB", sbi)], writes=[PSK(ps_o)])
        dst = ACC[:, :, q0:q0 + dil * 127 + 1:dil]
        src = psb[ps_o][:, 0:256].rearrange("p (a t) -> p a t", t=128)
        wk = [("ACCu", t)] + ([("ACCG", g)] if last else [])
        if g == 0:
            s.act(ACP(dst, src), reads=[PSK(ps_o), "ACCFIN"], writes=wk)
        else:
            oi = t % 4
            s.act(ACP(OST[oi][:, :], psb[ps_o][:, 0:256]), reads=[PSK(ps_o)], writes=[("OST", oi)])
            s.pool(TT(dst, dst, OST[oi][:, :].rearrange("p (a t) -> p a t", t=128), ALU.add),
                   reads=[("OST", oi), ("ACCG", g - 1)], writes=wk)
        if g == 2 and last:
            s.act(ACTF(RB[:, :], ACC[:, 1, :], AF.Ln), reads=[("ACCG", 2)], writes=["RB"])
            s.act(ACTF(RB[:, :], RB[:, :], AF.Exp, scale=-1.0), reads=["RB"], writes=["RB"])
            s.pool(TT(outbT[:, hh, :], ACC[:, 0, :], RB[:, :], ALU.mult),
                   reads=[("ACCG", 2), "RB"], writes=[("outbT", hh), "ACCFIN"])

    load_b(0)
    LAG = 2
    for t in range(len(qsets) + LAG):
        if t < len(qsets):
            b_stage1(t)
        if t >= LAG:
            b_stage2(t - LAG)
    s.fence()
    ar.reset(p3_mark)
    if stop_after in (4, 8):
        for h_ in range(4):
            s.dma("sp", DMA(dbg_ob[h_], outbT[:, h_, :]))
    if stop_after <= 4:
        s.emit()
        return nc, s

    WO_OFF = (ar.hi - 16 * D * 2) // 64 * 64
    WO = nc.alloc_sbuf_tensor_at("WO_top", [128, 16, D], BF16, offset=WO_OFF)
    SG = [ar.alloc("SG", [128, 2, 512], F32) for _ in range(3)]
    WPA = [ar.alloc("WPA", [128, 8, 256], BF16) for _ in range(2)]
    WPB = [ar.alloc("WPB", [128, 4, 256], BF16) for _ in range(2)]
    TU = [ar.alloc("TU", [128, 2, 512], F32) for _ in range(1)]

    def load_wp(jb):
        b = jb % 2
        srcA = w_pa[:, jb * 256:(jb + 1) * 256].rearrange("(kc p) c -> p kc c", p=128)
        for q in range(2):
            s.dma("pool", DMA(WPA[b][:, q * 4:(q + 1) * 4, :], srcA[:, q * 4:(q + 1) * 4, :]), writes=[("WPA", b, q)])
        s.dma("pool", DMA(WPB[b][:, :, :], w_pb[:, jb * 256:(jb + 1) * 256].rearrange("(kc p) c -> p kc c", p=128)),
              writes=[("WPB", b)])

    assert ar.cur <= WO_OFF, ("phase 5 temps overlap WO", ar.cur, WO_OFF)
    it = 0
    load_wp(0)
    for jb in range(8):
        if jb + 1 < 8:
            load_wp(jb + 1)
        for kc_ in (2 * jb, 2 * jb + 1):
            s.dma("pool", DMA(WO[:, kc_, :], w_out[kc_ * 128:(kc_ + 1) * 128, :]), writes=[("WO", kc_)])
        b = jb % 2
        for cc in range(2):
            f = jb * 2 + cc
            for tb in range(4):
                gi = it % 3
                ti = 0
                p1 = (it % 4) * 2
                p2 = p1 + 1
                it += 1
                s.dma("sp", DMA(SG[gi][:, 0, :], sga_d[f, :, tb * 512:(tb + 1) * 512]), writes=[("SG", gi, 0)])
                s.dma("sp", DMA(SG[gi][:, 1, :], sgb_d[f, :, tb * 512:(tb + 1) * 512]), writes=[("SG", gi, 1)])
                for kc in range(8):
                    s.pe(MM(psb[p1][:, :], WPA[b][:, kc, cc * 128:(cc + 1) * 128], outaT[:, kc, tb * 512:(tb + 1) * 512],
                            start=(kc == 0), stop=(kc == 7)), reads=[("WPA", b, kc // 4)], writes=[PSK(p1)])
                for kc in range(4):
                    s.pe(MM(psb[p2][:, :], WPB[b][:, kc, cc * 128:(cc + 1) * 128], outbT[:, kc, tb * 512:(tb + 1) * 512],
                            start=(kc == 0), stop=(kc == 3)), reads=[("WPB", b)], writes=[PSK(p2)])
                s.dve(TT(TU[ti][:, 0, :], psb[p1][:, :], SG[gi][:, 0, :], ALU.mult),
                      reads=[PSK(p1), ("SG", gi, 0)], writes=[("TU", ti, 0)])
                s.dve(TT(TU[ti][:, 1, :], psb[p2][:, :], SG[gi][:, 1, :], ALU.mult),
                      reads=[PSK(p2), ("SG", gi, 1)], writes=[("TU", ti, 1)])
                s.dve(TT(mixT[:, f, tb * 512:(tb + 1) * 512], TU[ti][:, 0, :], TU[ti][:, 1, :], ALU.add),
                      reads=[("TU", ti, 0), ("TU", ti, 1)], writes=[("mixT", f, tb)])
    s.fence()
    ar.reset(p6_mark)
    if stop_after in (5, 8):
        for f_ in range(16):
            s.dma("sp", DMA(dbg_mix[f_], mixT[:, f_, :]))
    if stop_after <= 5:
        s.emit()
        return nc, s

    LNP = ar.alloc("LNP", [128, 2, D], F32)
    XR = [ar.alloc("XR", [128, D], F32) for _ in range(2)]
    YY = [ar.alloc("YY", [128, D], F32) for _ in range(2)]
    XB = [ar.alloc("XB", [128, D], BF16) for _ in range(3)]
    XTS = [ar.alloc("XTS", [128, 16, 256], BF16) for _ in range(2)]
    STAT = [ar.alloc("STAT", [128, 24], F32) for _ in range(2)]
    MV = [ar.alloc("MV", [128, 2], F32) for _ in range(2)]
    RS = [ar.alloc("RS", [128, 1], F32) for _ in range(2)]
    assert ar.cur <= WO_OFF, ("phase 6 temps overlap WO", ar.cur, WO_OFF)
    s.dma("sp", DMA(LNP[:, 0, :], ln_d[0]), writes=["LNPg"])
    s.dma("sp", DMA(LNP[:, 1, :], ln_d[1]), writes=["LNPb"])

    def layer_norm(y, i, lnp, reads_y, out_ap, out_key):
        for q in range(4):
            s.dve(BNS(STAT[i][:, q * 6:(q + 1) * 6], y[:, q * 512:(q + 1) * 512]),
                  reads=reads_y, writes=[("STAT", i, q)])
        s.dve(BNA(MV[i][:, :], STAT[i][:, :]),
              reads=[("STAT", i, q) for q in range(4)], writes=[("MV", i)])
        s.act(ACTF(RS[i][:, :], MV[i][:, 1:2], AF.Sqrt, bias=epsl[:, 0:1], scale=1.0), reads=[("MV", i)], writes=[("RS", i)])
        s.dve(RECIP(RS[i][:, :], RS[i][:, :]), reads=[("RS", i)], writes=[("RS", i)])
        s.dve(TS(y[:, :], y[:, :], MV[i][:, 0:1], ALU.subtract, RS[i][:, 0:1], ALU.mult),
              reads=reads_y + [("MV", i), ("RS", i)], writes=reads_y)
        s.pool(TT(y[:, :], y[:, :], lnp[:, 0, :], ALU.mult), reads=reads_y + ["LNPg"], writes=reads_y)
        s.pool(TT(out_ap, y[:, :], lnp[:, 1, :], ALU.add), reads=reads_y + ["LNPb"], writes=reads_y)

    def p6_main(tt):
        i = tt % 2
        s.dma("sp", DMA(XR[i][:, :], xs[tt * 128:(tt + 1) * 128, :]), writes=[("XR", i)])
        for fb in range(4):
            bank = fb
            for kc in range(16):
                s.pe(MM(psb[bank][:, :], mixT[:, kc, tt * 128:(tt + 1) * 128], WO[:, kc, fb * 512:(fb + 1) * 512],
                        start=(kc == 0), stop=(kc == 15)), reads=[("WO", kc)], writes=[PSK(bank)])
            s.dve(STT(YY[i][:, fb * 512:(fb + 1) * 512], XR[i][:, fb * 512:(fb + 1) * 512], ALPHA, psb[bank][:, :],
                      ALU.mult, ALU.add), reads=[PSK(bank), ("XR", i)], writes=[("YY", i, fb)])
        ykeys = [("YY", i, fb) for fb in range(4)]
        if stop_after == 6:
            s.dma("sp", DMA(dbg_y[tt * 128:(tt + 1) * 128, :], YY[i][:, :]), reads=ykeys)
        layer_norm(YY[i], i, LNP, ykeys, YY[i][:, :], ("YY", i, 0))
        s.dma("pool", DMA(x1_d[tt * 128:(tt + 1) * 128, :], YY[i][:, :]), reads=ykeys)

    def p6_cast(tt):
        i = tt % 2
        ykeys = [("YY", i, fb) for fb in range(4)]
        s.act(ACP(XB[tt % 3][:, :], YY[i][:, :]), reads=ykeys, writes=[("XB", tt % 3)])

    def p6_tr(tt):
        i = tt % 2
        xi = (tt // 2) % 2
        for k in range(16):
            bank = 4 + k // 4
            col = (k % 4) * 128
            s.pe(MM(psb[bank][:, col:col + 128], XB[tt % 3][:, k * 128:(k + 1) * 128], ident_b[:, :]),
                 reads=[("XB", tt % 3)], writes=[PSK(bank)])
        for q in range(4):
            bank = 4 + q
            src = psb[bank][:, :].rearrange("p (k t) -> p k t", t=128)
            dst = XTS[xi][:, q * 4:(q + 1) * 4, (tt % 2) * 128:(tt % 2 + 1) * 128]
            if q % 2 == 0:
                s.act(ACP(dst, src), reads=[PSK(bank)], writes=[("XTS", xi, tt % 2, q)])
            else:
                s.dve(CP(dst, src), reads=[PSK(bank)], writes=[("XTS", xi, tt % 2, q)])
        if tt % 2 == 1:
            t0 = ((tt - 1) % 4) * 128
            for q in range(4):
                s.dma("pool", DMA(x1T_d[tt // 4].rearrange("p (kc t) -> p kc t", t=512)[:, q * 4:(q + 1) * 4, t0:t0 + 256],
                                XTS[xi][:, q * 4:(q + 1) * 4, :]),
                      reads=[("XTS", xi, a, q) for a in range(2)])

    for tt in range(16 + 2):
        if tt < 16:
            p6_main(tt)
        if 1 <= tt <= 16:
            p6_cast(tt - 1)
        if tt >= 2:
            p6_tr(tt - 2)
    issue_precast(100)
    s.dma_all.extend(precast_ops)
    s.fence()
    ar.reset(base_mark)
    if stop_after <= 6:
        s.emit()
        return nc, s

    X1T = [ar.alloc("X1T", [128, 8192], BF16) for _ in range(2)]
    HT = ar.alloc("HT", [128, NHC, 512], BF16)
    WG = [ar.alloc("WG", [128, 4096], BF16) for _ in range(3)]
    WU = [ar.alloc("WU", [128, 4096], BF16) for _ in range(3)]
    WD = [ar.alloc("WD", [128, 1024], BF16) for _ in range(6)]
    SGL = [ar.alloc("SGL", [128, 512], F32) for _ in range(2)]
    XR2 = [ar.alloc("XR2", [128, 1024], F32) for _ in range(4)]
    YY2 = [ar.alloc("YY2", [128, D], F32) for _ in range(4)]
    LNP2 = ar.alloc("LNP2", [128, 2, D], F32)
    STAT = [ar.alloc("STAT", [128, 24], F32) for _ in range(2)]
    MV = [ar.alloc("MV", [128, 2], F32) for _ in range(2)]
    RS = [ar.alloc("RS", [128, 1], F32) for _ in range(2)]

    wl = {"gu": 0, "d": 0}

    def load_gu(j):
        b = wl["gu"] % 3
        wl["gu"] += 1
        s.dma("sp", DMA(WG[b][:, :], wgb_d[j]), writes=[("WG", b)])
        s.dma("sp", DMA(WU[b][:, :], wub_d[j]), writes=[("WU", b)])
        return b

    def load_d(hc, half):
        b = wl["d"] % 6
        wl["d"] += 1
        s.dma("sp", DMA(WD[b][:, :], wdb_d[hc * 128:(hc + 1) * 128, half * 1024:(half + 1) * 1024]), writes=[("WD", b)])
        return b

    gu_seq = [(tb, j) for tb in range(4) for j in range(22)]
    d_seq = [(tb, pr, hc) for tb in range(4) for pr in range(2) for hc in range(NHC)]
    gu_bufs = {}
    d_bufs = {}
    gu_next = 0
    d_next = 0

    def ensure_gu(upto):
        nonlocal gu_next
        while gu_next <= upto and gu_next < len(gu_seq):
            gu_bufs[gu_next] = load_gu(gu_seq[gu_next][1])
            gu_next += 1

    def ensure_d(upto):
        nonlocal d_next
        while d_next <= upto and d_next < len(d_seq):
            d_bufs[d_next] = load_d(d_seq[d_next][2], d_seq[d_next][1])
            d_next += 1

    for q in range(4):
        s.dma("sp", DMA(X1T[0][:, q * 2048:(q + 1) * 2048], x1T_d[0][:, q * 2048:(q + 1) * 2048]), writes=[("X1T", 0, q)])
        if q == 0:
            ensure_gu(0)
    ensure_gu(1)
    s.dma("sp", DMA(LNP2[:, 0, :], ln_d[2]), writes=["LNPg"])
    s.dma("sp", DMA(LNP2[:, 1, :], ln_d[3]), writes=["LNPb"])
    sgi = 0
    lni = 0
    pending_ln = []
    for tb in range(4):
        xb_ = tb % 2
        if tb + 1 < 4:
            s.dma("sp", DMA(X1T[(tb + 1) % 2][:, :], x1T_d[tb + 1]),
                  writes=[("X1T", (tb + 1) % 2, q) for q in range(4)])
        for j in range(22):
            if pending_ln and j in (2, 7, 12, 17):
                pending_ln.pop(0)()
            gi = tb * 22 + j
            ensure_gu(gi + 2)
            wb_ = gu_bufs[gi]
            for cc in range(2):
                hc = j * 2 + cc
                pg = (hc % 2) * 2
                pu = pg + 1
                for kc in range(16):
                    s.pe(MM(psb[pg][:, :], WG[wb_][:, kc * 256 + cc * 128:kc * 256 + (cc + 1) * 128], X1T[xb_][:, kc * 512:(kc + 1) * 512],
                            start=(kc == 0), stop=(kc == 15)), reads=[("WG", wb_), ("X1T", xb_, kc // 4)], writes=[PSK(pg)])
                for kc in range(16):
                    s.pe(MM(psb[pu][:, :], WU[wb_][:, kc * 256 + cc * 128:kc * 256 + (cc + 1) * 128], X1T[xb_][:, kc * 512:(kc + 1) * 512],
                            start=(kc == 0), stop=(kc == 15)), reads=[("WU", wb_), ("X1T", xb_, kc // 4)], writes=[PSK(pu)])
                si = sgi % 2
                sgi += 1
                s.act(ACTF(SGL[si][:, :], psb[pg][:, :], AF.Silu), reads=[PSK(pg)], writes=[("SGL", si)])
                s.dve(TT(HT[:, hc, :], SGL[si][:, :], psb[pu][:, :], ALU.mult),
                      reads=[("SGL", si), PSK(pu)], writes=[("HT", hc)])
        for half in range(2):
            di0 = (tb * 2 + half) * NHC
            xr_idx = []
            for tl in range(4):
                tt = tb * 4 + tl
                i = lni % 4
                lni += 1
                xr_idx.append(i)
                s.dma("sp", DMA(XR2[i][:, :], x1_d[tt * 128:(tt + 1) * 128, half * 1024:(half + 1) * 1024]),
                      writes=[("XR2", i, 0), ("XR2", i, 1)])
            for hc in range(NHC):
                ensure_d(di0 + hc + 5)
                wb_ = d_bufs[di0 + hc]
                for tl in range(4):
                    for fbh in range(2):
                        bank = tl * 2 + fbh
                        s.pe(MM(psb[bank][:, :], HT[:, hc, tl * 128:(tl + 1) * 128], WD[wb_][:, fbh * 512:(fbh + 1) * 512],
                                start=(hc == 0), stop=(hc == NHC - 1)),
                             reads=[("WD", wb_), ("HT", hc)], writes=[PSK(bank)])
            for tl in range(4):
                tt = tb * 4 + tl
                i = xr_idx[tl]
                for fbh in range(2):
                    bank = tl * 2 + fbh
                    c0 = half * 1024 + fbh * 512
                    yk = ("YY2", tl, half * 2 + fbh)
                    if fbh == 0:
                        s.dve(STT(YY2[tl][:, c0:c0 + 512], XR2[i][:, fbh * 512:(fbh + 1) * 512], ALPHA,
                                  psb[bank][:, :], ALU.mult, ALU.add),
                              reads=[PSK(bank), ("XR2", i, 0)], writes=[yk])
                    else:
                        s.act(ACTF(YY2[tl][:, c0:c0 + 512], psb[bank][:, :], AF.Copy), reads=[PSK(bank)], writes=[yk])
                        s.act(ACTF(XR2[i][:, 512:1024], XR2[i][:, 512:1024], AF.Copy, scale=ALPHA),
                              reads=[("XR2", i, 1)], writes=[("XR2", i, 1)])
                        s.pool(TT(YY2[tl][:, c0:c0 + 512], YY2[tl][:, c0:c0 + 512], XR2[i][:, 512:1024], ALU.add),
                               reads=[yk, ("XR2", i, 1)], writes=[yk])
            if half == 1:
                for tl in range(4):
                    def ln2_job(tl=tl, tt=tb * 4 + tl):
                        ykeys = [("YY2", tl, q) for q in range(4)]
                        layer_norm(YY2[tl], tl % 2, LNP2, ykeys, YY2[tl][:, :], None)
                        s.dma("pool", DMA(out_d[tt * 128:(tt + 1) * 128, :], YY2[tl][:, :]), reads=ykeys)
                    pending_ln.append(ln2_job)
    while pending_ln:
        pending_ln.pop(0)()
    s.emit()
    return nc, s


def _host_consts():
    ident = np.eye(128, dtype=np.float32)
    rotm = np.zeros((128, 128), dtype=np.float32)
    for base in (0, 64):
        for i in range(32):
            rotm[base + 32 + i, base + i] = -1.0
            rotm[base + i, base + 32 + i] = 1.0
    t = np.arange(S)
    row_id = (t // 64).astype(np.float32)
    col_id = (t % 64).astype(np.float32)
    freqs = (10000.0 ** (-np.arange(32, dtype=np.float32) / 32.0)).astype(np.float32)
    ang_r = row_id[None, :] * freqs[:, None]
    ang_c = col_id[None, :] * freqs[:, None]
    C = np.concatenate([np.cos(ang_r), np.cos(ang_r), np.cos(ang_c), np.cos(ang_c)], axis=0).astype(np.float32)
    Sn = np.concatenate([np.sin(ang_r), np.sin(ang_r), np.sin(ang_c), np.sin(ang_c)], axis=0).astype(np.float32)
    i = np.arange(128)[:, None]
    j = np.arange(128)[None, :]
    bb = np.zeros((4, 128, 1152), dtype=np.float32)
    NEG = -30000.0
    for g in range(3):
        for hh in range(4):
            slope = 2.0 ** (-8.0 * (g * 4 + hh + 1) / 12.0)
            for ch in range(3):
                delta = (i - j) + (ch - 1) * 128
                valid = np.abs(delta) <= 64
                val = np.where(valid, -slope * B_DIL[g] * np.abs(delta), NEG).astype(np.float32)
                bb[hh, :, g * 384 + ch * 128:g * 384 + (ch + 1) * 128] = val
    return ident, rotm, C, Sn, bb


def _prep_inputs(inputs):
    x = np.asarray(inputs["x"], dtype=np.float32)
    ident, rotm, C, Sn, bb = _host_consts()
    bg = np.asarray(inputs["b_gate"], dtype=np.float32)[0]
    bgate_t = np.ascontiguousarray(bg.reshape(2, 16, 128).transpose(2, 0, 1).reshape(128, 32))
    qkg = np.ascontiguousarray(np.stack([np.asarray(inputs["q_norm_a"])[0], np.asarray(inputs["k_norm_a"])[0]], axis=1).astype(np.float32))
    ln = np.stack([np.asarray(inputs[k])[0] for k in ("ln1_g", "ln1_b", "ln2_g", "ln2_b")], axis=0).astype(np.float32)
    ln_t = np.ascontiguousarray(np.broadcast_to(ln[:, None, :], (4, 128, D)))
    shared = {
        "w_in": np.ascontiguousarray(np.asarray(inputs["w_in"], dtype=np.float32)[0]),
        "w_proj_a": np.ascontiguousarray(np.asarray(inputs["w_proj_a"], dtype=np.float32)[0]),
        "w_proj_b": np.ascontiguousarray(np.asarray(inputs["w_proj_b"], dtype=np.float32)[0]),
        "w_out": np.ascontiguousarray(np.asarray(inputs["w_out"], dtype=np.float32)[0]),
        "w_ffn_gate": np.ascontiguousarray(np.asarray(inputs["w_ffn_gate"], dtype=np.float32)[0]),
        "w_ffn_up": np.ascontiguousarray(np.asarray(inputs["w_ffn_up"], dtype=np.float32)[0]),
        "w_ffn_down": np.ascontiguousarray(np.asarray(inputs["w_ffn_down"], dtype=np.float32)[0]),
        "bgate_t": bgate_t, "qkg": qkg, "ln_t": ln_t, "ident": ident, "rotm": rotm, "bbias": bb,
    }
    in_maps = []
    for c in range(8):
        b, h = c // 2, c % 2
        if h == 0:
            xs = np.ascontiguousarray(x[b])
            Cc, Sc = C, Sn
        else:
            xs = np.ascontiguousarray(x[b, ::-1])
            Cc, Sc = np.ascontiguousarray(C[:, ::-1]), np.ascontiguousarray(Sn[:, ::-1])
        m = dict(shared)
        m["xs"] = xs
        m["ropeC"] = Cc
        m["ropeS"] = Sc
        in_maps.append(m)
    return in_maps


_CACHE = {}


def kernel(**inputs):
    if "nc" not in _CACHE:
        _CACHE["nc"] = build_program()[0]
    nc = _CACHE["nc"]
    in_maps = _prep_inputs(inputs)
    res = run_bass_kernel_spmd(nc, in_maps, core_ids=list(range(8)))
    out = np.empty((4, S, D), dtype=np.float32)
    for c in range(8):
        b, h = c // 2, c % 2
        o = np.asarray(res.results[c]["out"], dtype=np.float32)
        if h == 0:
            out[b, :T] = o
        else:
            out[b, T:] = o[::-1]
    return out
```

```python
import numpy as np
import concourse.bass as bass
import concourse.mybir as mybir
from concourse.bass_utils import run_bass_kernel_spmd

F32 = mybir.dt.float32
BF16 = mybir.dt.bfloat16
AF = mybir.ActivationFunctionType
ALU = mybir.AluOpType

D = 2048
S = 4096
T = 2048
HID = 5632
NHC = HID // 128
ALPHA = 2.0 ** 0.25
SCALE = 128.0 ** -0.5
RMS_EPS = 1e-6
LN_EPS = 1e-5
B_DIL = (1, 4, 16)
B_LEN = (2176, 2560, 4096)
B_M = (17, 5, 2)
B_NQ = (16, 4, 1)

ENGS = ("pe", "act", "dve", "pool", "sp")


class Op:
    __slots__ = ("eng", "fn", "dma", "deps", "signal", "tick", "slot", "expect",
                 "waits", "qidx", "seq")

    def __init__(self, eng, fn, dma):
        self.eng = eng
        self.fn = fn
        self.dma = dma
        self.deps = set()
        self.signal = False
        self.tick = 0
        self.slot = 0
        self.expect = 0
        self.waits = []
        self.qidx = 0


class Sched:
    def __init__(self, nc):
        self.nc = nc
        self.ops = []
        self.last_writer = {}
        self.readers = {}
        self.n_dma = {"sp": 16, "pool": 8, "act": 4}
        self.fence_pending = {}
        self.last_op = {}
        self.dma_all = []

    def add(self, eng, fn, reads=(), writes=(), dma=False):
        op = Op(eng, fn, dma)
        op.seq = len(self.ops)
        deps = set()
        for k in reads:
            w = self.last_writer.get(k)
            if w is not None:
                deps.add(w)
            self.readers.setdefault(k, []).append(op)
        for k in writes:
            w = self.last_writer.get(k)
            if w is not None:
                deps.add(w)
            for r in self.readers.get(k, ()):
                if r is not op:
                    deps.add(r)
        for k in writes:
            self.last_writer[k] = op
            self.readers[k] = []
        fp = self.fence_pending.get(eng)
        if fp:
            deps |= fp
            self.fence_pending[eng] = None
        op.deps = deps
        self.ops.append(op)
        if dma:
            self.dma_all.append(op)
        else:
            self.last_op[eng] = op
        return op

    def fence(self):
        s = set(self.last_op.values()) | set(self.dma_all)
        self.dma_all = []
        for e in ENGS:
            cur = self.fence_pending.get(e)
            self.fence_pending[e] = (cur | s) if cur else set(s)
        self.last_writer = {}
        self.readers = {}

    def pe(self, fn, reads=(), writes=()):
        return self.add("pe", fn, reads, writes)

    def act(self, fn, reads=(), writes=()):
        return self.add("act", fn, reads, writes)

    def dve(self, fn, reads=(), writes=()):
        return self.add("dve", fn, reads, writes)

    def pool(self, fn, reads=(), writes=()):
        return self.add("pool", fn, reads, writes)

    def dma(self, q, fn, reads=(), writes=()):
        return self.add(q, fn, reads, writes, dma=True)

    def emit(self):
        nc = self.nc
        ops = self.ops

        def skip(d, op):
            return d.eng == "pe" and op.eng == "pe" and not d.dma and not op.dma

        for op in ops:
            latest = {}
            keep = set()
            for d in op.deps:
                if skip(d, op):
                    continue
                if d.dma:
                    keep.add(d)
                else:
                    cur = latest.get(d.eng)
                    if cur is None or d.seq > cur.seq:
                        latest[d.eng] = d
            keep |= set(latest.values())
            op.deps = keep
            for d in keep:
                d.signal = True
        cnt = {e: 0 for e in ENGS}
        dcnt = {e: 0 for e in ENGS}
        for op in ops:
            if op.dma:
                i = dcnt[op.eng]
                dcnt[op.eng] += 1
                n = self.n_dma[op.eng]
                op.qidx = i
                op.slot = i % n
                op.expect = 16 * (i // n + 1)
                op.signal = True
            elif op.signal:
                cnt[op.eng] += 1
                op.tick = cnt[op.eng]
        esem = {e: nc.alloc_semaphore("es_" + e) for e in ENGS}
        dsem = {q: [nc.alloc_semaphore("ds_%s%d" % (q, i)) for i in range(self.n_dma[q])]
                for q in self.n_dma if dcnt[q] > 0}
        known = {e: {} for e in ENGS}
        per_eng = {e: [] for e in ENGS}
        for op in ops:
            need = {}
            for d in op.deps:
                if skip(d, op):
                    continue
                if d.dma:
                    key = ("d", d.eng, d.slot)
                    val = d.expect
                else:
                    key = ("e", d.eng)
                    val = d.tick
                if need.get(key, 0) < val:
                    need[key] = val
            if op.dma and op.qidx >= self.n_dma[op.eng]:
                key = ("d", op.eng, op.slot)
                val = op.expect - 16
                if need.get(key, 0) < val:
                    need[key] = val
            kn = known[op.eng]
            w = []
            for key, val in need.items():
                if kn.get(key, 0) >= val:
                    continue
                kn[key] = val
                sem = esem[key[1]] if key[0] == "e" else dsem[key[1]][key[2]]
                w.append((sem, val))
            op.waits = w
            per_eng[op.eng].append(op)
        final_waits = []
        for q in dsem:
            n = self.n_dma[q]
            tot = dcnt[q]
            for sl in range(n):
                uses = (tot - sl + n - 1) // n if tot > sl else 0
                if uses > 0:
                    final_waits.append((dsem[q][sl], 16 * uses))
        for e in ENGS:
            if cnt[e] > 0:
                final_waits.append((esem[e], cnt[e]))
        self.stats = {"n_ops": len(ops), "cnt": cnt, "dcnt": dcnt,
                      "n_waits": sum(len(o.waits) for o in ops),
                      "per_eng": {e: len(per_eng[e]) for e in ENGS}}
        handles = {"pe": "tensor", "act": "scalar", "dve": "vector",
                   "pool": "gpsimd", "sp": "sync"}
        with nc.Block() as block:
            for e in ENGS:
                lst = per_eng[e]
                is_sp = (e == "sp")
                if not lst and not is_sp:
                    continue

                def body(eng, lst=lst, e=e, is_sp=is_sp):
                    for op in lst:
                        for (sem, val) in op.waits:
                            eng.wait_ge(sem, val)
                        ins = op.fn(eng)
                        if op.signal:
                            if op.dma:
                                ins.then_inc(dsem[op.eng][op.slot], 16)
                            else:
                                ins.then_inc(esem[e], 1)
                    if is_sp:
                        for (sem, val) in final_waits:
                            eng.wait_ge(sem, val)

                getattr(block, handles[e])(body)


class Arena:
    def __init__(self, nc):
        self.nc = nc
        self.lo = (nc.sbuf_base + 63) // 64 * 64
        self.hi = nc.sbuf_top
        self.cur = self.lo
        self.n = 0

    def alloc(self, name, shape, dtype):
        esz = 2 if dtype == BF16 else 4
        nbytes = esz
        for d in shape[1:]:
            nbytes *= d
        nbytes = (nbytes + 63) // 64 * 64
        assert self.cur + nbytes <= self.hi, ("SBUF overflow", name, self.cur, nbytes, self.hi)
        self.n += 1
        t = self.nc.alloc_sbuf_tensor_at("%s_%d" % (name, self.n), list(shape), dtype, offset=self.cur)
        self.cur += nbytes
        return t

    def mark(self):
        return self.cur

    def reset(self, m):
        self.cur = m


def MM(out, lhsT, rhs, start=True, stop=True):
    return lambda e: e.matmul(out, lhsT=lhsT, rhs=rhs, start=start, stop=stop)


def TR(out, in_, ident):
    return lambda e: e.transpose(out, in_, ident)


def DMA(out, in_):
    return lambda e: e.dma_start(out=out, in_=in_)


def ACTF(out, in_, func, **kw):
    return lambda e: e.activation(out=out, in_=in_, func=func, **kw)


def TT(out, in0, in1, op):
    return lambda e: e.tensor_tensor(out=out, in0=in0, in1=in1, op=op)


def TS(out, in0, s1, op0, s2=None, op1=None):
    if op1 is None:
        return lambda e: e.tensor_scalar(out=out, in0=in0, scalar1=s1, scalar2=None, op0=op0)
    return lambda e: e.tensor_scalar(out=out, in0=in0, scalar1=s1, scalar2=s2, op0=op0, op1=op1)


def STT(out, in0, scalar, in1, op0, op1):
    return lambda e: e.scalar_tensor_tensor(out=out, in0=in0, scalar=scalar, in1=in1, op0=op0, op1=op1)


def CP(out, in_):
    return lambda e: e.tensor_copy(out=out, in_=in_)


def ACP(out, in_):
    return lambda e: e.copy(out=out, in_=in_)


def RECIP(out, in_):
    return lambda e: e.reciprocal(out=out, in_=in_)


def MEMSET(ap, v):
    return lambda e: e.memset(ap, v)


def BNS(out, in_):
    return lambda e: e.bn_stats(out=out, in_=in_)


def BNA(out, in_):
    return lambda e: e.bn_aggr(out=out, in_=in_)


def build_program(stop_after=99):
    nc = bass.Bass("TRN2", target_bir_lowering=False)
    dbg = stop_after < 99

    def din(name, shape, dt=F32):
        return nc.dram_tensor(name, list(shape), dt, kind="ExternalInput").ap()

    def dscr(name, shape, dt, tap=()):
        kind = "ExternalOutput" if (stop_after in tap) else "Internal"
        return nc.dram_tensor(name, list(shape), dt, kind=kind).ap()

    xs = din("xs", [S, D])
    w_in = din("w_in", [D, 10240])
    w_pa = din("w_proj_a", [1024, D])
    w_pb = din("w_proj_b", [512, D])
    w_out = din("w_out", [D, D])
    w_fg = din("w_ffn_gate", [D, HID])
    w_fu = din("w_ffn_up", [D, HID])
    w_fd = din("w_ffn_down", [HID, D])
    bgate_d = din("bgate_t", [128, 32])
    qkg_d = din("qkg", [128, 2])
    ln_d = din("ln_t", [4, 128, D])
    ident_d = din("ident", [128, 128])
    rotm_d = din("rotm", [128, 128])
    ropeC = din("ropeC", [128, S])
    ropeS = din("ropeS", [128, S])
    bbias_d = din("bbias", [4, 128, 1152])
    out_d = nc.dram_tensor("out", [T, D], F32, kind="ExternalOutput").ap()

    qaT_d = dscr("qaT", [8, 128, T], BF16, tap=(2,))
    kaT_d = dscr("kaT", [2, 128, S], BF16, tap=(2,))
    va_d = dscr("va", [2, 128, 4096], BF16, tap=(2,))
    qbT_d = dscr("qbT", [12, 128, T], BF16, tap=(2,))
    kbT_d = dscr("kbT", [12, 128, S], BF16, tap=(2,))
    vb_d = dscr("vb", [12, 128, 4096], BF16, tap=(2,))
    sga_d = dscr("sga", [16, 128, T], F32, tap=(2,))
    sgb_d = dscr("sgb", [16, 128, T], F32, tap=(2,))
    x1_d = dscr("x1", [T, D], F32, tap=(6, 8))
    x1T_d = dscr("x1T", [4, 128, 8192], BF16, tap=(6,))
    wgb_d = dscr("wgb", [22, 128, 4096], BF16)
    wub_d = dscr("wub", [22, 128, 4096], BF16)
    wdb_d = dscr("wdb", [HID, D], BF16)
    dbg_oa = dscr("dbg_oa", [8, 128, T], BF16, tap=(3, 4, 8))
    dbg_ob = dscr("dbg_ob", [4, 128, T], BF16, tap=(4, 8))
    dbg_mix = dscr("dbg_mix", [16, 128, T], BF16, tap=(5, 8))
    dbg_y = dscr("dbg_y", [T, D], F32, tap=(6,))

    s = Sched(nc)
    ar = Arena(nc)
    pbig = [nc.alloc_psum_tensor("pbig%d" % i, [128, 1024], F32) for i in range(4)]
    psb = [pbig[i // 2][:, (i % 2) * 512:(i % 2 + 1) * 512] for i in range(8)]

    def PSK(i):
        return ("ps", i)

    ident_b = ar.alloc("ident_b", [128, 128], BF16)
    ones_b = ar.alloc("ones_b", [128, 128], BF16)
    onesf = ar.alloc("onesf", [128, 128], F32)
    rotm = ar.alloc("rotm", [128, 128], F32)
    bgate = ar.alloc("bgate", [128, 32], F32)
    qkg = ar.alloc("qkg", [128, 2], F32)
    epsr = ar.alloc("epsr", [128, 1], F32)
    epsl = ar.alloc("epsl", [128, 1], F32)
    s.dma("pool", DMA(ident_b[:, :], ident_d[:, :]), writes=["ident_b"])
    s.dma("sp", DMA(rotm[:, :], rotm_d[:, :]), writes=["rotm"])
    s.dma("sp", DMA(bgate[:, :], bgate_d[:, :]), writes=["bgate"])
    s.dma("sp", DMA(qkg[:, :], qkg_d[:, :]), writes=["qkg"])
    s.pool(MEMSET(ones_b[:, :], 1.0), writes=["ones_b"])
    s.pool(MEMSET(onesf[:, :], 1.0 / 128.0), writes=["onesf"])
    s.pool(MEMSET(epsr[:, :], RMS_EPS), writes=["epsr"])
    s.pool(MEMSET(epsl[:, :], LN_EPS), writes=["epsl"])
    s.fence()
    base_mark = ar.mark()

    precast = []
    for j in range(22):
        for q in range(4):
            precast.append((wgb_d[j].rearrange("p (kc c) -> p kc c", c=256)[:, q * 4:(q + 1) * 4, :],
                            w_fg[:, j * 256:(j + 1) * 256].rearrange("(kc p) c -> p kc c", p=128)[:, q * 4:(q + 1) * 4, :], None))
            precast.append((wub_d[j].rearrange("p (kc c) -> p kc c", c=256)[:, q * 4:(q + 1) * 4, :],
                            w_fu[:, j * 256:(j + 1) * 256].rearrange("(kc p) c -> p kc c", p=128)[:, q * 4:(q + 1) * 4, :], None))
    for hc in range(0, NHC, 2):
        precast.append((wdb_d[hc * 128:(hc + 2) * 128, :], w_fd[hc * 128:(hc + 2) * 128, :], ("wdb", hc // 2)))
    precast_iter = iter(precast)
    precast_ops = []

    def issue_precast(n=1):
        for _ in range(n):
            job = next(precast_iter, None)
            if job is None:
                return
            op = s.dma("pool", DMA(job[0], job[1]))
            s.dma_all.remove(op)
            precast_ops.append(op)

    xT = ar.alloc("xT", [128, 16, S], BF16)
    p1_mark = ar.mark()
    xin32 = [ar.alloc("xin32", [128, D], F32) for _ in range(3)]
    xin16 = [ar.alloc("xin16", [128, D], BF16) for _ in range(2)]
    for tt in range(32):
        b3 = tt % 3
        b2 = tt % 2
        s.dma("sp", DMA(xin32[b3][:, :], xs[tt * 128:(tt + 1) * 128, :]), writes=[("x32", b3)])
        if tt % 2 == 0:
            s.dve(CP(xin16[b2][:, :], xin32[b3][:, :]), reads=[("x32", b3)], writes=[("x16", b2)])
        else:
            s.pool(CP(xin16[b2][:, :], xin32[b3][:, :]), reads=[("x32", b3)], writes=[("x16", b2)])
        for k in range(16):
            bank = (tt % 2) * 4 + k // 4
            col = (k % 4) * 128
            s.pe(MM(psb[bank][:, col:col + 128], xin16[b2][:, k * 128:(k + 1) * 128], ident_b[:, :]),
                 reads=[("x16", b2)], writes=[PSK(bank)])
        for q in range(4):
            bank = (tt % 2) * 4 + q
            src = psb[bank][:, :].rearrange("p (k t) -> p k t", t=128)
            dst = xT[:, q * 4:(q + 1) * 4, tt * 128:(tt + 1) * 128]
            if q % 2 == 0:
                s.act(ACP(dst, src), reads=[PSK(bank)], writes=[("xT", tt, q)])
            else:
                s.dve(CP(dst, src), reads=[PSK(bank)], writes=[("xT", tt, q)])
    s.fence()
    ar.reset(p1_mark)
    if stop_after <= 1:
        xT_dbg = nc.dram_tensor("dbg_xT", [128, 16, S], BF16, kind="ExternalOutput").ap()
        s.dma("sp", DMA(xT_dbg[:, :, :], xT[:, :, :]))
        s.emit()
        return nc, s

    wbuf = [ar.alloc("wbuf", [128, 16, 256], BF16) for _ in range(3)]
    f32t = {n: [ar.alloc(n, [128, 512], F32) for _ in range(2)] for n in ("sq", "y", "r", "t1", "t2", "cc", "ss")}
    stg = [ar.alloc("stg", [128, 512], BF16) for _ in range(4)]
    gstg = [ar.alloc("gstg", [128, 512], F32) for _ in range(3)]
    vstg = [ar.alloc("vstg", [128, 256], BF16) for _ in range(4)]
    cnt = {"stg": 0, "gstg": 0, "vstg": 0, "acc": 0, "rr": 0, "ev": 0}

    def load_wblock(j):
        b = j % 3
        src = w_in[:, j * 256:(j + 1) * 256].rearrange("(kc p) c -> p kc c", p=128)
        for q in range(4):
            s.dma("pool", DMA(wbuf[b][:, q * 4:(q + 1) * 4, :], src[:, q * 4:(q + 1) * 4, :]), writes=[("wbuf", b, q)])

    def fm_accum(j, cc, tb):
        b = j % 3
        bank = cnt["acc"] % 4
        cnt["acc"] += 1
        for kc in range(16):
            s.pe(MM(psb[bank][:, :], wbuf[b][:, kc, cc * 128:(cc + 1) * 128], xT[:, kc, tb * 512:(tb + 1) * 512],
                    start=(kc == 0), stop=(kc == 15)),
                 reads=[("wbuf", b, kc // 4)], writes=[PSK(bank)])
        return bank

    rr_pending = []

    def rms_rope(bank, tb, gcol, scale, dst_dram):
        i = cnt["rr"] % 2
        cnt["rr"] += 1
        sq, y, r, t1, t2, cc_, ss_ = (f32t[n][i] for n in ("sq", "y", "r", "t1", "t2", "cc", "ss"))
        K = lambda n: (n, i)
        s.dma("sp", DMA(cc_[:, :], ropeC[:, tb * 512:(tb + 1) * 512]), writes=[K("cc")])
        s.dma("sp", DMA(ss_[:, :], ropeS[:, tb * 512:(tb + 1) * 512]), writes=[K("ss")])
        s.act(ACTF(sq[:, :], psb[bank][:, :], AF.Square), reads=[PSK(bank)], writes=[K("sq")])
        s.act(ACTF(y[:, :], psb[bank][:, :], AF.Copy, scale=qkg[:, gcol:gcol + 1]), reads=[PSK(bank)], writes=[K("y")])

        def stage_b():
            pa = 4 + i
            pbk = 6 + i
            s.pe(MM(psb[pa][:, :], onesf[:, :], sq[:, :]), reads=[K("sq")], writes=[PSK(pa)])
            s.pe(MM(psb[pbk][:, :], rotm[:, :], y[:, :]), reads=[K("y")], writes=[PSK(pbk)])
            s.act(ACTF(r[:, :], psb[pa][:, :], AF.Ln, bias=epsr[:, 0:1], scale=1.0), reads=[PSK(pa)], writes=[K("r")])
            s.act(ACTF(r[:, :], r[:, :], AF.Exp, scale=-0.5), reads=[K("r")], writes=[K("r")])
            s.pool(TT(t1[:, :], y[:, :], cc_[:, :], ALU.mult), reads=[K("y"), K("cc")], writes=[K("t1")])
            s.dve(TT(t2[:, :], psb[pbk][:, :], ss_[:, :], ALU.mult), reads=[PSK(pbk), K("ss")], writes=[K("t2")])
            s.pool(TT(t1[:, :], t1[:, :], t2[:, :], ALU.add), reads=[K("t1"), K("t2")], writes=[K("t1")])
            si = cnt["stg"] % 4
            cnt["stg"] += 1
            s.dve(STT(stg[si][:, :], t1[:, :], scale, r[:, :], ALU.mult, ALU.mult),
                  reads=[K("t1"), K("r")], writes=[("stg", si)])
            s.dma("sp", DMA(dst_dram, stg[si][:, :]), reads=[("stg", si)])

        rr_pending.append(stage_b)
        if len(rr_pending) > 1:
            rr_pending.pop(0)()

    def rr_flush():
        while rr_pending:
            rr_pending.pop(0)()

    def plain_evac(bank, scale, dst_dram):
        si = cnt["stg"] % 4
        cnt["stg"] += 1
        use_act = (cnt["ev"] % 2 == 0) or scale != 1.0
        cnt["ev"] += 1
        if use_act:
            s.act(ACTF(stg[si][:, :], psb[bank][:, :], AF.Copy, scale=scale), reads=[PSK(bank)], writes=[("stg", si)])
        else:
            s.dve(CP(stg[si][:, :], psb[bank][:, :]), reads=[PSK(bank)], writes=[("stg", si)])
        s.dma("sp", DMA(dst_dram, stg[si][:, :]), reads=[("stg", si)])

    def gate_evac(bank, bcol, dst_dram):
        gi = cnt["gstg"] % 3
        cnt["gstg"] += 1
        s.act(ACTF(gstg[gi][:, :], psb[bank][:, :], AF.Sigmoid, bias=bgate[:, bcol:bcol + 1], scale=1.0),
              reads=[PSK(bank)], writes=[("gstg", gi)])
        s.dma("sp", DMA(dst_dram, gstg[gi][:, :]), reads=[("gstg", gi)])

    def tm_block(j, dil, M, dst, h0):
        b = j % 3
        for r in range(dil):
            for m in range(M):
                bank = cnt["acc"] % 4
                cnt["acc"] += 1
                t0 = r + dil * 128 * m
                for kc in range(16):
                    s.pe(MM(psb[bank][:, 0:256], xT[:, kc, t0:t0 + dil * 127 + 1:dil], wbuf[b][:, kc, :],
                            start=(kc == 0), stop=(kc == 15)),
                         reads=[("wbuf", b, kc // 4)], writes=[PSK(bank)])
                vi = cnt["vstg"] % 4
                cnt["vstg"] += 1
                s.dve(CP(vstg[vi][:, :], psb[bank][:, 0:256]), reads=[PSK(bank)], writes=[("vstg", vi)])
                for a in range(2):
                    s.dma("sp", DMA(dst[h0 + a, :, (r * M + m) * 128:(r * M + m + 1) * 128], vstg[vi][:, a * 128:(a + 1) * 128]),
                          reads=[("vstg", vi)])

    NBLK = 40
    load_wblock(0)
    load_wblock(1)
    for j in range(NBLK):
        if j + 2 < NBLK:
            load_wblock(j + 2)
        issue_precast(3)
        c0 = j * 256
        if c0 < 1024:
            for cc in range(2):
                h = (c0 // 128) + cc
                for tb in range(4):
                    bank = fm_accum(j, cc, tb)
                    rms_rope(bank, tb, 0, SCALE, qaT_d[h, :, tb * 512:(tb + 1) * 512])
        elif c0 < 1280:
            for cc in range(2):
                for tb in range(8):
                    bank = fm_accum(j, cc, tb)
                    rms_rope(bank, tb, 1, 1.0, kaT_d[cc, :, tb * 512:(tb + 1) * 512])
        elif c0 < 1536:
            rr_flush()
            tm_block(j, 1, 32, va_d, 0)
        elif c0 < 3072:
            for cc in range(2):
                h = (c0 - 1536) // 128 + cc
                for tb in range(4):
                    bank = fm_accum(j, cc, tb)
                    plain_evac(bank, SCALE, qbT_d[h, :, tb * 512:(tb + 1) * 512])
        elif c0 < 4608:
            for cc in range(2):
                h = (c0 - 3072) // 128 + cc
                ntb = 8 if h >= 8 else 5
                for tb in range(ntb):
                    bank = fm_accum(j, cc, tb)
                    plain_evac(bank, 1.0, kbT_d[h, :, tb * 512:(tb + 1) * 512])
        elif c0 < 6144:
            h0 = (c0 - 4608) // 128
            g = h0 // 4
            tm_block(j, B_DIL[g], B_M[g], vb_d, h0)
        else:
            which = 0 if c0 < 8192 else 1
            dst = sga_d if which == 0 else sgb_d
            for cc in range(2):
                f = ((c0 - 6144) % 2048) // 128 + cc
                for tb in range(4):
                    bank = fm_accum(j, cc, tb)
                    gate_evac(bank, which * 16 + f, dst[f, :, tb * 512:(tb + 1) * 512])
    s.fence()
    ar.reset(base_mark)
    if stop_after <= 2:
        s.emit()
        return nc, s

    mixT = ar.alloc("mixT", [128, 16, T], BF16)
    p6_mark = ar.mark()
    outaT = ar.alloc("outaT", [128, 8, T], BF16)
    outbT = ar.alloc("outbT", [128, 4, T], BF16)
    p3_mark = ar.mark()
    KT = [ar.alloc("KT", [128, S], BF16) for _ in range(2)]
    VV = [ar.alloc("VV", [128, 4096], BF16) for _ in range(2)]
    QT = [ar.alloc("QT", [128, T], BF16) for _ in range(2)]
    PT2 = [ar.alloc("PT2", [128, 1024], BF16) for _ in range(4)]
    PS = [ar.alloc("PS", [128, 512], BF16) for _ in range(4)]
    RD = [ar.alloc("RD", [128, 512], F32) for _ in range(2)]

    def load_k(g):
        for q in range(4):
            s.dma("sp", DMA(KT[g][:, q * 1024:(q + 1) * 1024], kaT_d[g][:, q * 1024:(q + 1) * 1024]), writes=[("KT", g, q)])

    def load_v(g):
        for q in range(4):
            s.dma("sp", DMA(VV[g][:, q * 1024:(q + 1) * 1024], va_d[g][:, q * 1024:(q + 1) * 1024]), writes=[("VV", g, q)])

    def load_q(h):
        s.dma("sp", DMA(QT[h % 2][:, :], qaT_d[h]), writes=[("QT", h % 2)])

    load_q(0)
    load_k(0)
    load_v(0)
    load_k(1)
    load_v(1)
    blk = 0
    pcn = 0
    LAG3 = 2
    for h in range(8):
        g = h // 4
        if h + 1 < 8:
            load_q(h + 1)
        qt = QT[h % 2]
        for qb in range(4):
            issue_precast(3)
            po = 4 + (blk % 2)
            pd = 6 + (blk % 2)
            blk += 1
            bufs = {}
            for pc in range(16 + LAG3):
                if pc < 16:
                    bk = pcn % 2
                    pi = pcn % 4
                    pcn += 1
                    bufs[pc] = pi
                    for e2 in range(2):
                        c = 2 * pc + e2
                        s.pe(MM(pbig[bk][:, e2 * 512:(e2 + 1) * 512], KT[g][:, c * 128:(c + 1) * 128],
                                qt[:, qb * 512:(qb + 1) * 512]),
                             reads=[("KT", g, c // 8), ("QT", h % 2)], writes=[("pbig", bk)])
                    s.act(ACTF(PT2[pi][:, :], pbig[bk][:, :], AF.Exp), reads=[("pbig", bk)], writes=[("PT2", pi)])
                    s.dve(TT(PS[pi][:, :], PT2[pi][:, 0:512], PT2[pi][:, 512:1024], ALU.add),
                          reads=[("PT2", pi)], writes=[("PS", pi)])
                    if pc % 2 == 1:
                        ppi = bufs[pc - 1]
                        s.dve(TT(PS[pi][:, :], PS[pi][:, :], PS[ppi][:, :], ALU.add),
                              reads=[("PS", pi), ("PS", ppi)], writes=[("PS", pi)])
                if pc >= LAG3:
                    pp = pc - LAG3
                    pi = bufs[pp]
                    for e2 in range(2):
                        cc = 2 * pp + e2
                        s.pe(MM(psb[po][:, :], VV[g][:, cc * 128:(cc + 1) * 128], PT2[pi][:, e2 * 512:(e2 + 1) * 512],
                                start=(cc == 0), stop=(cc == 31)),
                             reads=[("VV", g, cc // 8), ("PT2", pi)], writes=[PSK(po)])
                    if pp % 2 == 1:
                        s.pe(MM(psb[pd][:, :], ones_b[:, :], PS[pi][:, :], start=(pp == 1), stop=(pp == 15)),
                             reads=[("PS", pi)], writes=[PSK(pd)])
            ri = blk % 2
            s.dve(RECIP(RD[ri][:, :], psb[pd][:, :]), reads=[PSK(pd)], writes=[("RD", ri)])
            s.dve(TT(outaT[:, h, qb * 512:(qb + 1) * 512], psb[po][:, :], RD[ri][:, :], ALU.mult),
                  reads=[PSK(po), ("RD", ri)], writes=[("outaT", h, qb)])
    s.fence()
    ar.reset(p3_mark)
    if stop_after in (3, 4, 8):
        for h_ in range(8):
            s.dma("sp", DMA(dbg_oa[h_], outaT[:, h_, :]))
    if stop_after <= 3:
        s.emit()
        return nc, s

    QB = [ar.alloc("QB", [128, T], BF16) for _ in range(2)]
    KB = [ar.alloc("KB", [128, S], BF16) for _ in range(2)]
    VB = [ar.alloc("VB", [128, 4096], BF16) for _ in range(2)]
    BIAS = [ar.alloc("BIAS", [128, 1152], F32) for _ in range(2)]
    SB = [ar.alloc("SB", [128, 384], F32) for _ in range(4)]
    PB = [ar.alloc("PB", [128, 384], BF16) for _ in range(4)]
    ACC = ar.alloc("ACC", [128, 2, T], F32)
    RB = ar.alloc("RB", [128, T], F32)
    OST = [ar.alloc("OST", [128, 256], F32) for _ in range(4)]
    combos = [(hh, g) for hh in range(4) for g in range(3)]

    def load_b(ci):
        hh, g = combos[ci]
        hd = g * 4 + hh
        b = ci % 2
        dil, L, M = B_DIL[g], B_LEN[g], B_M[g]
        s.dma("sp", DMA(QB[b][:, :], qbT_d[hd]), writes=[("QB", b)])
        s.dma("sp", DMA(KB[b][:, 0:L], kbT_d[hd, :, 0:L]), writes=[("KB", b)])
        s.dma("sp", DMA(VB[b][:, 0:dil * M * 128], vb_d[hd, :, 0:dil * M * 128]), writes=[("VB", b)])
        if g == 0:
            s.dma("sp", DMA(BIAS[hh % 2][:, :], bbias_d[hh]), writes=[("BIAS", hh % 2)])

    qsets = []
    for ci, (hh, g) in enumerate(combos):
        dil, M, NQ = B_DIL[g], B_M[g], B_NQ[g]
        for r in range(dil):
            for n in range(NQ):
                qsets.append((ci, hh, g, r, n, (r == 0 and n == 0), (r == dil - 1 and n == NQ - 1)))

    def b_stage1(t):
        ci, hh, g, r, n, first, last = qsets[t]
        b = ci % 2
        dil, M = B_DIL[g], B_M[g]
        ms = [m for m in (n - 1, n, n + 1) if 0 <= m < M]
        ps_s = t % 4
        sbi = t % 4
        q0 = r + dil * 128 * n
        qap = QB[b][:, q0:q0 + dil * 127 + 1:dil]
        for m in ms:
            col = (m - n + 1) * 128
            k0 = r + dil * 128 * m
            kap = KB[b][:, k0:k0 + dil * 127 + 1:dil]
            s.pe(MM(psb[ps_s][:, col:col + 128], kap, qap), reads=[("KB", b), ("QB", b)], writes=[PSK(ps_s)])
        c_lo = (ms[0] - n + 1) * 128
        c_hi = (ms[-1] - n + 2) * 128
        s.dve(TT(SB[sbi][:, c_lo:c_hi], psb[ps_s][:, c_lo:c_hi],
                 BIAS[hh % 2][:, g * 384 + c_lo:g * 384 + c_hi], ALU.add),
              reads=[PSK(ps_s), ("BIAS", hh % 2)], writes=[("SB", sbi)])
        s.act(ACTF(PB[sbi][:, c_lo:c_hi], SB[sbi][:, c_lo:c_hi], AF.Exp), reads=[("SB", sbi)], writes=[("PB", sbi)])

    def b_stage2(t):
        ci, hh, g, r, n, first, last = qsets[t]
        if first and ci + 1 < len(combos):
            load_b(ci + 1)
        b = ci % 2
        dil, M = B_DIL[g], B_M[g]
        ms = [m for m in (n - 1, n, n + 1) if 0 <= m < M]
        sbi = t % 4
        ps_o = 4 + (t % 4)
        q0 = r + dil * 128 * n
        for idx, m in enumerate(ms):
            col = (m - n + 1) * 128
            s.pe(MM(psb[ps_o][:, 0:128], VB[b][:, (r * M + m) * 128:(r * M + m + 1) * 128], PB[sbi][:, col:col + 128],
                    start=(idx == 0), stop=(idx == len(ms) - 1)),
                 reads=[("VB", b), ("PB", sbi)], writes=[PSK(ps_o)])
        for idx, m in enumerate(ms):
            col = (m - n + 1) * 128
            s.pe(MM(psb[ps_o][:, 128:256], ones_b[:, :], PB[sbi][:, col:col + 128],
                    start=(idx == 0), stop=(idx == len(ms) - 1)),
                 reads=[("PB", sbi)], writes=[PSK(ps_o)])
        dst = ACC[:, :, q0:q0 + dil * 127 + 1:dil]
        src = psb[ps_o][:, 0:256].rearrange("p (a t) -> p a t", t=128)
        wk = [("ACCu", t)] + ([("ACCG", g)] if last else [])
        if g == 0:
            s.act(ACP(dst, src), reads=[PSK(ps_o), "ACCFIN"], writes=wk)
        else:
            oi = t % 4
            s.act(ACP(OST[oi][:, :], psb[ps_o][:, 0:256]), reads=[PSK(ps_o)], writes=[("OST", oi)])
            s.pool(TT(dst, dst, OST[oi][:, :].rearrange("p (a t) -> p a t", t=128), ALU.add),
                   reads=[("OST", oi), ("ACCG", g - 1)], writes=wk)
        if g == 2 and last:
            s.act(ACTF(RB[:, :], ACC[:, 1, :], AF.Ln), reads=[("ACCG", 2)], writes=["RB"])
            s.act(ACTF(RB[:, :], RB[:, :], AF.Exp, scale=-1.0), reads=["RB"], writes=["RB"])
            s.pool(TT(outbT[:, hh, :], ACC[:, 0, :], RB[:, :], ALU.mult),
                   reads=[("ACCG", 2), "RB"], writes=[("outbT", hh), "ACCFIN"])

    load_b(0)
    LAG = 2
    for t in range(len(qsets) + LAG):
        if t < len(qsets):
            b_stage1(t)
        if t >= LAG:
            b_stage2(t - LAG)
    s.fence()
    ar.reset(p3_mark)
    if stop_after in (4, 8):
        for h_ in range(4):
            s.dma("sp", DMA(dbg_ob[h_], outbT[:, h_, :]))
    if stop_after <= 4:
        s.emit()
        return nc, s

    WO_OFF = (ar.hi - 16 * D * 2) // 64 * 64
    WO = nc.alloc_sbuf_tensor_at("WO_top", [128, 16, D], BF16, offset=WO_OFF)
    SG = [ar.alloc("SG", [128, 2, 512], F32) for _ in range(3)]
    WPA = [ar.alloc("WPA", [128, 8, 256], BF16) for _ in range(2)]
    WPB = [ar.alloc("WPB", [128, 4, 256], BF16) for _ in range(2)]
    TU = [ar.alloc("TU", [128, 2, 512], F32) for _ in range(1)]

    def load_wp(jb):
        b = jb % 2
        srcA = w_pa[:, jb * 256:(jb + 1) * 256].rearrange("(kc p) c -> p kc c", p=128)
        for q in range(2):
            s.dma("pool", DMA(WPA[b][:, q * 4:(q + 1) * 4, :], srcA[:, q * 4:(q + 1) * 4, :]), writes=[("WPA", b, q)])
        s.dma("pool", DMA(WPB[b][:, :, :], w_pb[:, jb * 256:(jb + 1) * 256].rearrange("(kc p) c -> p kc c", p=128)),
              writes=[("WPB", b)])

    assert ar.cur <= WO_OFF, ("phase 5 temps overlap WO", ar.cur, WO_OFF)
    it = 0
    load_wp(0)
    for jb in range(8):
        if jb + 1 < 8:
            load_wp(jb + 1)
        for kc_ in (2 * jb, 2 * jb + 1):
            s.dma("pool", DMA(WO[:, kc_, :], w_out[kc_ * 128:(kc_ + 1) * 128, :]), writes=[("WO", kc_)])
        b = jb % 2
        for cc in range(2):
            f = jb * 2 + cc
            for tb in range(4):
                gi = it % 3
                ti = 0
                p1 = (it % 4) * 2
                p2 = p1 + 1
                it += 1
                s.dma("sp", DMA(SG[gi][:, 0, :], sga_d[f, :, tb * 512:(tb + 1) * 512]), writes=[("SG", gi, 0)])
                s.dma("sp", DMA(SG[gi][:, 1, :], sgb_d[f, :, tb * 512:(tb + 1) * 512]), writes=[("SG", gi, 1)])
                for kc in range(8):
                    s.pe(MM(psb[p1][:, :], WPA[b][:, kc, cc * 128:(cc + 1) * 128], outaT[:, kc, tb * 512:(tb + 1) * 512],
                            start=(kc == 0), stop=(kc == 7)), reads=[("WPA", b, kc // 4)], writes=[PSK(p1)])
                for kc in range(4):
                    s.pe(MM(psb[p2][:, :], WPB[b][:, kc, cc * 128:(cc + 1) * 128], outbT[:, kc, tb * 512:(tb + 1) * 512],
                            start=(kc == 0), stop=(kc == 3)), reads=[("WPB", b)], writes=[PSK(p2)])
                s.dve(TT(TU[ti][:, 0, :], psb[p1][:, :], SG[gi][:, 0, :], ALU.mult),
                      reads=[PSK(p1), ("SG", gi, 0)], writes=[("TU", ti, 0)])
                s.dve(TT(TU[ti][:, 1, :], psb[p2][:, :], SG[gi][:, 1, :], ALU.mult),
                      reads=[PSK(p2), ("SG", gi, 1)], writes=[("TU", ti, 1)])
                s.dve(TT(mixT[:, f, tb * 512:(tb + 1) * 512], TU[ti][:, 0, :], TU[ti][:, 1, :], ALU.add),
                      reads=[("TU", ti, 0), ("TU", ti, 1)], writes=[("mixT", f, tb)])
    s.fence()
    ar.reset(p6_mark)
    if stop_after in (5, 8):
        for f_ in range(16):
            s.dma("sp", DMA(dbg_mix[f_], mixT[:, f_, :]))
    if stop_after <= 5:
        s.emit()
        return nc, s

    LNP = ar.alloc("LNP", [128, 2, D], F32)
    XR = [ar.alloc("XR", [128, D], F32) for _ in range(2)]
    YY = [ar.alloc("YY", [128, D], F32) for _ in range(2)]
    XB = [ar.alloc("XB", [128, D], BF16) for _ in range(3)]
    XTS = [ar.alloc("XTS", [128, 16, 256], BF16) for _ in range(2)]
    STAT = [ar.alloc("STAT", [128, 24], F32) for _ in range(2)]
    MV = [ar.alloc("MV", [128, 2], F32) for _ in range(2)]
    RS = [ar.alloc("RS", [128, 1], F32) for _ in range(2)]
    assert ar.cur <= WO_OFF, ("phase 6 temps overlap WO", ar.cur, WO_OFF)
    s.dma("sp", DMA(LNP[:, 0, :], ln_d[0]), writes=["LNPg"])
    s.dma("sp", DMA(LNP[:, 1, :], ln_d[1]), writes=["LNPb"])

    def layer_norm(y, i, lnp, reads_y, out_ap, out_key):
        for q in range(4):
            s.dve(BNS(STAT[i][:, q * 6:(q + 1) * 6], y[:, q * 512:(q + 1) * 512]),
                  reads=reads_y, writes=[("STAT", i, q)])
        s.dve(BNA(MV[i][:, :], STAT[i][:, :]),
              reads=[("STAT", i, q) for q in range(4)], writes=[("MV", i)])
        s.act(ACTF(RS[i][:, :], MV[i][:, 1:2], AF.Sqrt, bias=epsl[:, 0:1], scale=1.0), reads=[("MV", i)], writes=[("RS", i)])
        s.dve(RECIP(RS[i][:, :], RS[i][:, :]), reads=[("RS", i)], writes=[("RS", i)])
        s.dve(TS(y[:, :], y[:, :], MV[i][:, 0:1], ALU.subtract, RS[i][:, 0:1], ALU.mult),
              reads=reads_y + [("MV", i), ("RS", i)], writes=reads_y)
        s.pool(TT(y[:, :], y[:, :], lnp[:, 0, :], ALU.mult), reads=reads_y + ["LNPg"], writes=reads_y)
        s.pool(TT(out_ap, y[:, :], lnp[:, 1, :], ALU.add), reads=reads_y + ["LNPb"], writes=reads_y)

    def p6_main(tt):
        i = tt % 2
        s.dma("sp", DMA(XR[i][:, :], xs[tt * 128:(tt + 1) * 128, :]), writes=[("XR", i)])
        for fb in range(4):
            bank = fb
            for kc in range(16):
                s.pe(MM(psb[bank][:, :], mixT[:, kc, tt * 128:(tt + 1) * 128], WO[:, kc, fb * 512:(fb + 1) * 512],
                        start=(kc == 0), stop=(kc == 15)), reads=[("WO", kc)], writes=[PSK(bank)])
            s.dve(STT(YY[i][:, fb * 512:(fb + 1) * 512], XR[i][:, fb * 512:(fb + 1) * 512], ALPHA, psb[bank][:, :],
                      ALU.mult, ALU.add), reads=[PSK(bank), ("XR", i)], writes=[("YY", i, fb)])
        ykeys = [("YY", i, fb) for fb in range(4)]
        if stop_after == 6:
            s.dma("sp", DMA(dbg_y[tt * 128:(tt + 1) * 128, :], YY[i][:, :]), reads=ykeys)
        layer_norm(YY[i], i, LNP, ykeys, YY[i][:, :], ("YY", i, 0))
        s.dma("pool", DMA(x1_d[tt * 128:(tt + 1) * 128, :], YY[i][:, :]), reads=ykeys)

    def p6_cast(tt):
        i = tt % 2
        ykeys = [("YY", i, fb) for fb in range(4)]
        s.act(ACP(XB[tt % 3][:, :], YY[i][:, :]), reads=ykeys, writes=[("XB", tt % 3)])

    def p6_tr(tt):
        i = tt % 2
        xi = (tt // 2) % 2
        for k in range(16):
            bank = 4 + k // 4
            col = (k % 4) * 128
            s.pe(MM(psb[bank][:, col:col + 128], XB[tt % 3][:, k * 128:(k + 1) * 128], ident_b[:, :]),
                 reads=[("XB", tt % 3)], writes=[PSK(bank)])
        for q in range(4):
            bank = 4 + q
            src = psb[bank][:, :].rearrange("p (k t) -> p k t", t=128)
            dst = XTS[xi][:, q * 4:(q + 1) * 4, (tt % 2) * 128:(tt % 2 + 1) * 128]
            if q % 2 == 0:
                s.act(ACP(dst, src), reads=[PSK(bank)], writes=[("XTS", xi, tt % 2, q)])
            else:
                s.dve(CP(dst, src), reads=[PSK(bank)], writes=[("XTS", xi, tt % 2, q)])
        if tt % 2 == 1:
            t0 = ((tt - 1) % 4) * 128
            for q in range(4):
                s.dma("pool", DMA(x1T_d[tt // 4].rearrange("p (kc t) -> p kc t", t=512)[:, q * 4:(q + 1) * 4, t0:t0 + 256],
                                XTS[xi][:, q * 4:(q + 1) * 4, :]),
                      reads=[("XTS", xi, a, q) for a in range(2)])

    for tt in range(16 + 2):
        if tt < 16:
            p6_main(tt)
        if 1 <= tt <= 16:
            p6_cast(tt - 1)
        if tt >= 2:
            p6_tr(tt - 2)
    issue_precast(100)
    s.dma_all.extend(precast_ops)
    s.fence()
    ar.reset(base_mark)
    if stop_after <= 6:
        s.emit()
        return nc, s

    X1T = [ar.alloc("X1T", [128, 8192], BF16) for _ in range(2)]
    HT = ar.alloc("HT", [128, NHC, 512], BF16)
    WG = [ar.alloc("WG", [128, 4096], BF16) for _ in range(3)]
    WU = [ar.alloc("WU", [128, 4096], BF16) for _ in range(3)]
    WD = [ar.alloc("WD", [128, 1024], BF16) for _ in range(6)]
    SGL = [ar.alloc("SGL", [128, 512], F32) for _ in range(2)]
    XR2 = [ar.alloc("XR2", [128, 1024], F32) for _ in range(4)]
    YY2 = [ar.alloc("YY2", [128, D], F32) for _ in range(4)]
    LNP2 = ar.alloc("LNP2", [128, 2, D], F32)
    STAT = [ar.alloc("STAT", [128, 24], F32) for _ in range(2)]
    MV = [ar.alloc("MV", [128, 2], F32) for _ in range(2)]
    RS = [ar.alloc("RS", [128, 1], F32) for _ in range(2)]

    wl = {"gu": 0, "d": 0}

    def load_gu(j):
        b = wl["gu"] % 3
        wl["gu"] += 1
        s.dma("sp", DMA(WG[b][:, :], wgb_d[j]), writes=[("WG", b)])
        s.dma("sp", DMA(WU[b][:, :], wub_d[j]), writes=[("WU", b)])
        return b

    def load_d(hc, half):
        b = wl["d"] % 6
        wl["d"] += 1
        s.dma("sp", DMA(WD[b][:, :], wdb_d[hc * 128:(hc + 1) * 128, half * 1024:(half + 1) * 1024]), writes=[("WD", b)])
        return b

    gu_seq = [(tb, j) for tb in range(4) for j in range(22)]
    d_seq = [(tb, pr, hc) for tb in range(4) for pr in range(2) for hc in range(NHC)]
    gu_bufs = {}
    d_bufs = {}
    gu_next = 0
    d_next = 0

    def ensure_gu(upto):
        nonlocal gu_next
        while gu_next <= upto and gu_next < len(gu_seq):
            gu_bufs[gu_next] = load_gu(gu_seq[gu_next][1])
            gu_next += 1

    def ensure_d(upto):
        nonlocal d_next
        while d_next <= upto and d_next < len(d_seq):
            d_bufs[d_next] = load_d(d_seq[d_next][2], d_seq[d_next][1])
            d_next += 1

    for q in range(4):
        s.dma("sp", DMA(X1T[0][:, q * 2048:(q + 1) * 2048], x1T_d[0][:, q * 2048:(q + 1) * 2048]), writes=[("X1T", 0, q)])
        if q == 0:
            ensure_gu(0)
    ensure_gu(1)
    s.dma("sp", DMA(LNP2[:, 0, :], ln_d[2]), writes=["LNPg"])
    s.dma("sp", DMA(LNP2[:, 1, :], ln_d[3]), writes=["LNPb"])
    sgi = 0
    lni = 0
    pending_ln = []
    for tb in range(4):
        xb_ = tb % 2
        if tb + 1 < 4:
            s.dma("sp", DMA(X1T[(tb + 1) % 2][:, :], x1T_d[tb + 1]),
                  writes=[("X1T", (tb + 1) % 2, q) for q in range(4)])
        for j in range(22):
            if pending_ln and j in (2, 7, 12, 17):
                pending_ln.pop(0)()
            gi = tb * 22 + j
            ensure_gu(gi + 2)
            wb_ = gu_bufs[gi]
            for cc in range(2):
                hc = j * 2 + cc
                pg = (hc % 2) * 2
                pu = pg + 1
                for kc in range(16):
                    s.pe(MM(psb[pg][:, :], WG[wb_][:, kc * 256 + cc * 128:kc * 256 + (cc + 1) * 128], X1T[xb_][:, kc * 512:(kc + 1) * 512],
                            start=(kc == 0), stop=(kc == 15)), reads=[("WG", wb_), ("X1T", xb_, kc // 4)], writes=[PSK(pg)])
                for kc in range(16):
                    s.pe(MM(psb[pu][:, :], WU[wb_][:, kc * 256 + cc * 128:kc * 256 + (cc + 1) * 128], X1T[xb_][:, kc * 512:(kc + 1) * 512],
                            start=(kc == 0), stop=(kc == 15)), reads=[("WU", wb_), ("X1T", xb_, kc // 4)], writes=[PSK(pu)])
                si = sgi % 2
                sgi += 1
                s.act(ACTF(SGL[si][:, :], psb[pg][:, :], AF.Silu), reads=[PSK(pg)], writes=[("SGL", si)])
                s.dve(TT(HT[:, hc, :], SGL[si][:, :], psb[pu][:, :], ALU.mult),
                      reads=[("SGL", si), PSK(pu)], writes=[("HT", hc)])
        for half in range(2):
            di0 = (tb * 2 + half) * NHC
            xr_idx = []
            for tl in range(4):
                tt = tb * 4 + tl
                i = lni % 4
                lni += 1
                xr_idx.append(i)
                s.dma("sp", DMA(XR2[i][:, :], x1_d[tt * 128:(tt + 1) * 128, half * 1024:(half + 1) * 1024]),
                      writes=[("XR2", i, 0), ("XR2", i, 1)])
            for hc in range(NHC):
                ensure_d(di0 + hc + 5)
                wb_ = d_bufs[di0 + hc]
                for tl in range(4):
                    for fbh in range(2):
                        bank = tl * 2 + fbh
                        s.pe(MM(psb[bank][:, :], HT[:, hc, tl * 128:(tl + 1) * 128], WD[wb_][:, fbh * 512:(fbh + 1) * 512],
                                start=(hc == 0), stop=(hc == NHC - 1)),
                             reads=[("WD", wb_), ("HT", hc)], writes=[PSK(bank)])
            for tl in range(4):
                tt = tb * 4 + tl
                i = xr_idx[tl]
                for fbh in range(2):
                    bank = tl * 2 + fbh
                    c0 = half * 1024 + fbh * 512
                    yk = ("YY2", tl, half * 2 + fbh)
                    if fbh == 0:
                        s.dve(STT(YY2[tl][:, c0:c0 + 512], XR2[i][:, fbh * 512:(fbh + 1) * 512], ALPHA,
                                  psb[bank][:, :], ALU.mult, ALU.add),
                              reads=[PSK(bank), ("XR2", i, 0)], writes=[yk])
                    else:
                        s.act(ACTF(YY2[tl][:, c0:c0 + 512], psb[bank][:, :], AF.Copy), reads=[PSK(bank)], writes=[yk])
                        s.act(ACTF(XR2[i][:, 512:1024], XR2[i][:, 512:1024], AF.Copy, scale=ALPHA),
                              reads=[("XR2", i, 1)], writes=[("XR2", i, 1)])
                        s.pool(TT(YY2[tl][:, c0:c0 + 512], YY2[tl][:, c0:c0 + 512], XR2[i][:, 512:1024], ALU.add),
                               reads=[yk, ("XR2", i, 1)], writes=[yk])
            if half == 1:
                for tl in range(4):
                    def ln2_job(tl=tl, tt=tb * 4 + tl):
                        ykeys = [("YY2", tl, q) for q in range(4)]
                        layer_norm(YY2[tl], tl % 2, LNP2, ykeys, YY2[tl][:, :], None)
                        s.dma("pool", DMA(out_d[tt * 128:(tt + 1) * 128, :], YY2[tl][:, :]), reads=ykeys)
                    pending_ln.append(ln2_job)
    while pending_ln:
        pending_ln.pop(0)()
    s.emit()
    return nc, s


def _host_consts():
    ident = np.eye(128, dtype=np.float32)
    rotm = np.zeros((128, 128), dtype=np.float32)
    for base in (0, 64):
        for i in range(32):
            rotm[base + 32 + i, base + i] = -1.0
            rotm[base + i, base + 32 + i] = 1.0
    t = np.arange(S)
    row_id = (t // 64).astype(np.float32)
    col_id = (t % 64).astype(np.float32)
    freqs = (10000.0 ** (-np.arange(32, dtype=np.float32) / 32.0)).astype(np.float32)
    ang_r = row_id[None, :] * freqs[:, None]
    ang_c = col_id[None, :] * freqs[:, None]
    C = np.concatenate([np.cos(ang_r), np.cos(ang_r), np.cos(ang_c), np.cos(ang_c)], axis=0).astype(np.float32)
    Sn = np.concatenate([np.sin(ang_r), np.sin(ang_r), np.sin(ang_c), np.sin(ang_c)], axis=0).astype(np.float32)
    i = np.arange(128)[:, None]
    j = np.arange(128)[None, :]
    bb = np.zeros((4, 128, 1152), dtype=np.float32)
    NEG = -30000.0
    for g in range(3):
        for hh in range(4):
            slope = 2.0 ** (-8.0 * (g * 4 + hh + 1) / 12.0)
            for ch in range(3):
                delta = (i - j) + (ch - 1) * 128
                valid = np.abs(delta) <= 64
                val = np.where(valid, -slope * B_DIL[g] * np.abs(delta), NEG).astype(np.float32)
                bb[hh, :, g * 384 + ch * 128:g * 384 + (ch + 1) * 128] = val
    return ident, rotm, C, Sn, bb


def _prep_inputs(inputs):
    x = np.asarray(inputs["x"], dtype=np.float32)
    ident, rotm, C, Sn, bb = _host_consts()
    bg = np.asarray(inputs["b_gate"], dtype=np.float32)[0]
    bgate_t = np.ascontiguousarray(bg.reshape(2, 16, 128).transpose(2, 0, 1).reshape(128, 32))
    qkg = np.ascontiguousarray(np.stack([np.asarray(inputs["q_norm_a"])[0], np.asarray(inputs["k_norm_a"])[0]], axis=1).astype(np.float32))
    ln = np.stack([np.asarray(inputs[k])[0] for k in ("ln1_g", "ln1_b", "ln2_g", "ln2_b")], axis=0).astype(np.float32)
    ln_t = np.ascontiguousarray(np.broadcast_to(ln[:, None, :], (4, 128, D)))
    shared = {
        "w_in": np.ascontiguousarray(np.asarray(inputs["w_in"], dtype=np.float32)[0]),
        "w_proj_a": np.ascontiguousarray(np.asarray(inputs["w_proj_a"], dtype=np.float32)[0]),
        "w_proj_b": np.ascontiguousarray(np.asarray(inputs["w_proj_b"], dtype=np.float32)[0]),
        "w_out": np.ascontiguousarray(np.asarray(inputs["w_out"], dtype=np.float32)[0]),
        "w_ffn_gate": np.ascontiguousarray(np.asarray(inputs["w_ffn_gate"], dtype=np.float32)[0]),
        "w_ffn_up": np.ascontiguousarray(np.asarray(inputs["w_ffn_up"], dtype=np.float32)[0]),
        "w_ffn_down": np.ascontiguousarray(np.asarray(inputs["w_ffn_down"], dtype=np.float32)[0]),
        "bgate_t": bgate_t, "qkg": qkg, "ln_t": ln_t, "ident": ident, "rotm": rotm, "bbias": bb,
    }
    in_maps = []
    for c in range(8):
        b, h = c // 2, c % 2
        if h == 0:
            xs = np.ascontiguousarray(x[b])
            Cc, Sc = C, Sn
        else:
            xs = np.ascontiguousarray(x[b, ::-1])
            Cc, Sc = np.ascontiguousarray(C[:, ::-1]), np.ascontiguousarray(Sn[:, ::-1])
        m = dict(shared)
        m["xs"] = xs
        m["ropeC"] = Cc
        m["ropeS"] = Sc
        in_maps.append(m)
    return in_maps


_CACHE = {}


def kernel(**inputs):
    if "nc" not in _CACHE:
        _CACHE["nc"] = build_program()[0]
    nc = _CACHE["nc"]
    in_maps = _prep_inputs(inputs)
    res = run_bass_kernel_spmd(nc, in_maps, core_ids=list(range(8)))
    out = np.empty((4, S, D), dtype=np.float32)
    for c in range(8):
        b, h = c // 2, c % 2
        o = np.asarray(res.results[c]["out"], dtype=np.float32)
        if h == 0:
            out[b, :T] = o
        else:
            out[b, T:] = o[::-1]
    return out
```
